# Optimizing a Trainium2 kernel written in Bass

```python
import math
import jax, jax.numpy as jnp
from jax import lax
import numpy as np


D_MODEL = 2048
BATCH = 2
SEQ = 4096
DEPTH = 1

PLE_DIM = 256
D_FF = 5632
CONV_WIDTH = D_MODEL
SC_KERNEL = 3
SSD_HEADS = 32
SSD_HEAD_DIM = 64
SSD_WIDTH = SSD_HEADS * SSD_HEAD_DIM
SSD_STATE = 128
SSD_GROUPS = 4
SSD_CONV = 4
CHUNK = 128
SSD_BC = SSD_GROUPS * SSD_STATE
SSD_XBC = SSD_WIDTH + 2 * SSD_BC
MIX_WIDTH = CONV_WIDTH + SSD_WIDTH
IN_COLS = 3 * CONV_WIDTH + SSD_WIDTH + SSD_XBC + SSD_HEADS
EPS = 1e-6

kernel_name = 'hymba_shortconv_ssd_macaron_ple'


def rmsnorm(x, w):
    xf = x.astype(jnp.float32)
    y = xf * lax.rsqrt(jnp.mean(xf * xf, axis=-1, keepdims=True) + EPS)
    return (y * w.astype(jnp.float32)).astype(x.dtype)


def gated_group_rmsnorm(y, z, w):
    g = (y * jax.nn.silu(z.astype(jnp.float32)))
    shp = g.shape
    g = g.reshape(shp[:-1] + (SSD_GROUPS, shp[-1] // SSD_GROUPS))
    g = g * lax.rsqrt(jnp.mean(g * g, axis=-1, keepdims=True) + EPS)
    return g.reshape(shp) * w.astype(jnp.float32)


def causal_dwconv(x, w):
    k = w.shape[0]
    c = x.shape[-1]
    return lax.conv_general_dilated(
        x, w[:, None, :].astype(x.dtype), window_strides=(1,), padding=[(k - 1, 0)],
        dimension_numbers=('NWC', 'WIO', 'NWC'), feature_group_count=c)


def swiglu(x, w_in, w_out):
    g, u = jnp.split(x @ w_in, 2, axis=-1)
    return (jax.nn.silu(g) * u) @ w_out


def segsum_exp(a_cs):
    n = a_cs.shape[-1]
    mask = jnp.tril(jnp.ones((n, n), dtype=bool))
    diff = a_cs[..., :, None] - a_cs[..., None, :]
    return jnp.where(mask, jnp.exp(jnp.where(mask, diff, 0.0)), 0.0)


def ssd_chunked(x, dt, a, bm, cm):
    b, L = x.shape[0], x.shape[1]
    nc = L // CHUNK
    r = SSD_HEADS // SSD_GROUPS
    X = (x * dt[..., None]).reshape(b, nc, CHUNK, SSD_GROUPS, r, SSD_HEAD_DIM)
    A = (dt * a).reshape(b, nc, CHUNK, SSD_GROUPS, r).transpose(0, 1, 3, 4, 2)
    Bc = bm.reshape(b, nc, CHUNK, SSD_GROUPS, SSD_STATE)
    Cc = cm.reshape(b, nc, CHUNK, SSD_GROUPS, SSD_STATE)
    a_cs = jnp.cumsum(A, axis=-1)
    Lmat = segsum_exp(a_cs)
    CB = jnp.einsum('bclgn,bcsgn->bcgls', Cc, Bc)
    y_diag = jnp.einsum('bcgls,bcgrls,bcsgrp->bclgrp', CB, Lmat, X)
    decay_states = jnp.exp(a_cs[..., -1:] - a_cs)
    states = jnp.einsum('bclgn,bcgrl,bclgrp->bcgrpn', Bc, decay_states, X)
    chunk_decay = jnp.exp(a_cs[..., -1])

    def step(s, inp):
        st, dec = inp
        return s * dec[..., None, None] + st, s

    s0 = jnp.zeros((b, SSD_GROUPS, r, SSD_HEAD_DIM, SSD_STATE), jnp.float32)
    _, prev = lax.scan(step, s0, (states.transpose(1, 0, 2, 3, 4, 5), chunk_decay.transpose(1, 0, 2, 3)))
    prev = prev.transpose(1, 0, 2, 3, 4, 5)
    y_off = jnp.einsum('bclgn,bcgrpn,bcgrl->bclgrp', Cc, prev, jnp.exp(a_cs))
    return (y_diag + y_off).reshape(b, L, SSD_HEADS, SSD_HEAD_DIM)


def hybrid_mixer(u, w_in, sc_conv_w, ssd_conv_w, ssd_conv_b, dt_bias, a_log, d_skip, ssd_norm_w, w_out):
    b, L, _ = u.shape
    zall = u @ w_in
    cuts = [CONV_WIDTH, 2 * CONV_WIDTH, 3 * CONV_WIDTH, 3 * CONV_WIDTH + SSD_WIDTH,
            3 * CONV_WIDTH + SSD_WIDTH + SSD_XBC]
    sc_b, sc_c, sc_x, z, xbc, dt_raw = jnp.split(zall, cuts, axis=-1)
    y_a = sc_b * causal_dwconv(sc_c * sc_x, sc_conv_w)
    xbc = jax.nn.silu(causal_dwconv(xbc, ssd_conv_w) + ssd_conv_b.astype(xbc.dtype))
    xs, bm, cm = jnp.split(xbc, [SSD_WIDTH, SSD_WIDTH + SSD_BC], axis=-1)
    f32 = jnp.float32
    dt = jax.nn.softplus(dt_raw.astype(f32) + dt_bias.astype(f32))
    a = -jnp.exp(a_log.astype(f32))
    x4 = xs.astype(f32).reshape(b, L, SSD_HEADS, SSD_HEAD_DIM)
    y = ssd_chunked(x4, dt, a,
                    bm.astype(f32).reshape(b, L, SSD_GROUPS, SSD_STATE),
                    cm.astype(f32).reshape(b, L, SSD_GROUPS, SSD_STATE))
    y = (y + d_skip.astype(f32)[:, None] * x4).reshape(b, L, SSD_WIDTH)
    y_b = gated_group_rmsnorm(y, z, ssd_norm_w).astype(u.dtype)
    return jnp.concatenate([y_a, y_b], axis=-1) @ w_out


def setup_inputs(seed: int = 0) -> dict:
    key = jax.random.key(seed)
    ks = jax.random.split(key, 24)
    f32 = jnp.float32

    def nrm(k, shape, fan_in):
        return jax.random.normal(k, shape, f32) * (fan_in ** -0.5)

    def gain(k, shape):
        return 1.0 + 0.02 * jax.random.normal(k, shape, f32)

    dt0 = jnp.exp(jax.random.uniform(ks[10], (DEPTH, SSD_HEADS), f32, math.log(1e-3), math.log(1e-1)))
    return {
        'x': jax.random.normal(ks[0], (BATCH, SEQ, D_MODEL), f32),
        'p': jax.random.normal(ks[1], (DEPTH, BATCH, SEQ, PLE_DIM), f32),
        'ffn1_norm': gain(ks[2], (DEPTH, D_MODEL)),
        'ffn1_w_in': nrm(ks[3], (DEPTH, D_MODEL, 2 * D_FF), D_MODEL),
        'ffn1_w_out': nrm(ks[4], (DEPTH, D_FF, D_MODEL), D_FF),
        'mix_norm': gain(ks[5], (DEPTH, D_MODEL)),
        'mix_w_in': nrm(ks[6], (DEPTH, D_MODEL, IN_COLS), D_MODEL),
        'sc_conv_w': nrm(ks[7], (DEPTH, SC_KERNEL, CONV_WIDTH), SC_KERNEL),
        'ssd_conv_w': nrm(ks[8], (DEPTH, SSD_CONV, SSD_XBC), SSD_CONV),
        'ssd_conv_b': 0.02 * jax.random.normal(ks[9], (DEPTH, SSD_XBC), f32),
        'ssd_dt_bias': dt0 + jnp.log(-jnp.expm1(-dt0)),
        'ssd_a_log': jnp.log(jax.random.uniform(ks[11], (DEPTH, SSD_HEADS), f32, 1.0, 16.0)),
        'ssd_d': gain(ks[12], (DEPTH, SSD_HEADS)),
        'ssd_norm': gain(ks[13], (DEPTH, SSD_WIDTH)),
        'mix_w_out': nrm(ks[14], (DEPTH, MIX_WIDTH, D_MODEL), MIX_WIDTH),
        'ffn2_norm': gain(ks[15], (DEPTH, D_MODEL)),
        'ffn2_w_in': nrm(ks[16], (DEPTH, D_MODEL, 2 * D_FF), D_MODEL),
        'ffn2_w_out': nrm(ks[17], (DEPTH, D_FF, D_MODEL), D_FF),
        'ple_norm': gain(ks[18], (DEPTH, D_MODEL)),
        'ple_w_gate': nrm(ks[19], (DEPTH, D_MODEL, D_MODEL), D_MODEL),
        'ple_w_proj': nrm(ks[20], (DEPTH, PLE_DIM, D_MODEL), PLE_DIM),
        'final_norm': gain(ks[21], (D_MODEL,)),
    }


def reference(x, p, ffn1_norm, ffn1_w_in, ffn1_w_out, mix_norm, mix_w_in, sc_conv_w, ssd_conv_w,
              ssd_conv_b, ssd_dt_bias, ssd_a_log, ssd_d, ssd_norm, mix_w_out, ffn2_norm, ffn2_w_in,
              ffn2_w_out, ple_norm, ple_w_gate, ple_w_proj, final_norm):
    h = x
    for i in range(DEPTH):
        h = h + 0.5 * swiglu(rmsnorm(h, ffn1_norm[i]), ffn1_w_in[i], ffn1_w_out[i])
        h = h + hybrid_mixer(rmsnorm(h, mix_norm[i]), mix_w_in[i], sc_conv_w[i], ssd_conv_w[i],
                             ssd_conv_b[i], ssd_dt_bias[i], ssd_a_log[i], ssd_d[i], ssd_norm[i],
                             mix_w_out[i])
        h = h + 0.5 * swiglu(rmsnorm(h, ffn2_norm[i]), ffn2_w_in[i], ffn2_w_out[i])
        gate = jax.nn.sigmoid(rmsnorm(h, ple_norm[i]) @ ple_w_gate[i])
        h = h + gate * (p[i] @ ple_w_proj[i])
    return rmsnorm(h, final_norm)
```

```python
import numpy as np
from contextlib import ExitStack
import concourse.bass as bass
import concourse.mybir as mybir
from concourse.bass_utils import run_bass_kernel_spmd

F32 = mybir.dt.float32
BF16 = mybir.dt.bfloat16
U8 = mybir.dt.uint8
AF = mybir.ActivationFunctionType
ALU = mybir.AluOpType

ENGS = ("pe", "act", "dve", "pool", "sp")


class Op:
    __slots__ = ("eng", "fn", "r", "w", "dma", "sem", "deps", "signal", "sigval", "waits", "idx", "bar", "inc")

    def __init__(self, eng, fn, r, w, dma, sem, bar=False, inc=16):
        self.inc = inc
        self.eng = eng
        self.fn = fn
        self.r = tuple(r)
        self.w = tuple(w)
        self.dma = dma
        self.sem = sem
        self.bar = bar
        self.deps = ()
        self.signal = False
        self.sigval = 0
        self.waits = ()


class Prog:
    def __init__(self, nc):
        self.nc = nc
        self.ops = []

    def add(self, eng, fn, r=(), w=(), dma=False, sem=None, inc=16):
        o = Op(eng, fn, r, w, dma, sem, inc=inc)
        self.ops.append(o)
        return o

    def barrier(self, engs=ENGS):
        for e in engs:
            self.ops.append(Op(e, None, (), (), False, None, bar=True))

    def analyze(self):
        last_w = {}
        readers = {}
        last_eng = {}
        last_dma = {}
        for i, o in enumerate(self.ops):
            o.idx = i
            deps = {}
            if o.bar:
                for p in last_eng.values():
                    deps[p.idx] = p
                for p in last_dma.values():
                    deps[p.idx] = p
            for k in o.r:
                p = last_w.get(k)
                if p is not None:
                    deps[p.idx] = p
            for k in o.w:
                p = last_w.get(k)
                if p is not None:
                    deps[p.idx] = p
                rd = readers.get(k)
                if rd:
                    for p in rd.values():
                        deps[p.idx] = p
            for k in o.r:
                rd = readers.setdefault(k, {})
                rd[("d", i) if o.dma else o.eng] = o
            for k in o.w:
                last_w[k] = o
                readers[k] = {}
            deps.pop(i, None)
            dl = []
            for p in deps.values():
                if (not p.dma) and p.eng == "pe" and o.eng == "pe" and not o.dma:
                    continue
                if p.fn is None:
                    continue
                dl.append(p)
                if not p.dma:
                    p.signal = True
            o.deps = dl
            if o.dma:
                last_dma[o.sem] = o
            elif o.fn is not None:
                last_eng[o.eng] = o
        cnt = {e: 0 for e in ENGS}
        dcnt = {}
        for o in self.ops:
            if o.dma:
                dcnt[o.sem] = dcnt.get(o.sem, 0) + o.inc
                o.sigval = dcnt[o.sem]
            elif o.signal:
                cnt[o.eng] += 1
                o.sigval = cnt[o.eng]
        self.dma_keys = list(dcnt.keys())
        waited = {e: {} for e in ENGS}
        nw = 0
        for o in self.ops:
            need = {}
            for p in o.deps:
                key = ("d", p.sem) if p.dma else ("e", p.eng)
                if p.sigval > need.get(key, 0):
                    need[key] = p.sigval
            ws = []
            wd = waited[o.eng]
            for key, v in need.items():
                if v > wd.get(key, 0):
                    wd[key] = v
                    ws.append((key, v))
            o.waits = ws
            nw += len(ws)
        self.stats = dict(n_ops=len(self.ops), n_waits=nw, sig=cnt, ndma_sems=len(dcnt))

    def emit(self):
        nc = self.nc
        self.analyze()
        byeng = {e: [] for e in ENGS}
        for o in self.ops:
            byeng[o.eng].append(o)
        with ExitStack() as st:
            esem = {e: st.enter_context(nc.semaphore("se_" + e)) for e in ENGS}
            dsem = {k: st.enter_context(nc.semaphore("sd_%d" % i)) for i, k in enumerate(self.dma_keys)}
            block = st.enter_context(nc.Block())

            def mk(ename):
                def f(eng):
                    for o in byeng[ename]:
                        for (key, v) in o.waits:
                            s = dsem[key[1]] if key[0] == "d" else esem[key[1]]
                            eng.wait_ge(s, v)
                        if o.fn is None:
                            continue
                        ins = o.fn(eng)
                        if o.dma:
                            ins.then_inc(dsem[o.sem], o.inc)
                        elif o.signal:
                            ins.then_inc(esem[ename], 1)
                return f

            block.tensor(mk("pe"))
            block.scalar(mk("act"))
            block.vector(mk("dve"))
            block.gpsimd(mk("pool"))
            block.sync(mk("sp"))


D = 2048
DT = 16
DFF = 5632
FT = 44
T = 1024
HALO = 8
TH = T + HALO
T3 = [(0, 344), (344, 344), (688, 344)]
T2 = [(0, 512), (512, 512)]
EPS = 1e-6
NCORES = 8

OFF_H = 0
OFF_U = 66048
OFF_HID = 99072
OFF_RING = 132096
NSLOT = 6
SLOT = 8192
OFF_R2 = OFF_RING + 3 * SLOT
OFF_C = OFF_RING + NSLOT * SLOT
OFF_T = OFF_C + 6144
SB_BYTES = 212800

C_IDF = 0
C_ONES = 128
C_TRI = 256
C_PAR = 384
P_NW = 0
P_SCW = 80
P_SSDW = 128
P_SSDB = 224
P_DTB = 248
P_ALOG = 280
P_DSK = 312
P_SSDN = 344
P_SEG = 360
P_EPS = 368
P_ONE = 369
NPAR = 384
NCST = C_PAR + NPAR
B_ID = 0
B_MEAN = 128
B_NEG = 256
NCSTB = 768


def build_program(stage=5, ncores=NCORES, skip_mixer=False):
    nc = bass.Bass("TRN2", target_bir_lowering=False)
    dram = {}

    def din(name, shape):
        dram[name] = nc.dram_tensor(name, shape, F32, kind="ExternalInput").ap()
        return dram[name]

    x_d = din("xc", [TH, D])
    p_d = din("pc", [T, 256])
    cst_d = din("cst", [128, NCST])
    cstb_d = din("cstb", [128, NCSTB])
    wdt_d = din("wdt", [128, 16 * 32])
    w1i = din("ffn1_w_in", [D, 2 * DFF])
    w1o = din("ffn1_w_out", [DFF, D])
    wmi = din("mix_w_in", [D, 11296])
    wmo = din("mix_w_out", [4096, D])
    w2i = din("ffn2_w_in", [D, 2 * DFF])
    w2o = din("ffn2_w_out", [DFF, D])
    wpg = din("ple_w_gate", [D, D])
    wpp = din("ple_w_proj", [256, D])
    out_d = nc.dram_tensor("out", [T, D], F32, kind="ExternalOutput").ap()
    h_sp = nc.dram_tensor("h_spill", [128, DT * T], F32).ap()
    ya_sp = nc.dram_tensor("ya_spill", [128, DT * T], BF16).ap()
    cc_in = nc.dram_tensor("cc_in", [128, 2080], F32)
    cc_out = nc.dram_tensor("cc_out", [ncores * 128, 2080], F32)

    P = Prog(nc)
    st = ExitStack()
    raw = st.enter_context(nc.sbuf_tensor("raw", [128, SB_BYTES], U8))
    ps = st.enter_context(nc.psum_tensor("ps", [128, 8, 512], F32))

    def V(off, shape, dt):
        n = int(np.prod(shape[1:])) * (4 if dt == F32 else 2)
        v = raw[:, off:off + n].bitcast(dt)
        if len(shape) == 3:
            v = v.rearrange("p (a b) -> p a b", a=shape[1])
        elif len(shape) == 4:
            v = v.rearrange("p (a b c) -> p a b c", a=shape[1], b=shape[2])
        return v

    def psb(b):
        return ps[:, b, :].bitcast(BF16)

    h_fm = V(OFF_H, [128, DT, TH], F32)
    u_fm = V(OFF_U, [128, DT, TH], BF16)
    hid = V(OFF_HID, [128, 16, TH], BF16)
    cst = V(OFF_C, [128, NCST], F32)
    cstb = V(OFF_C + 3072, [128, NCSTB], BF16)
    wdt = V(OFF_C + 4608, [128, 16, 32], BF16)
    idf = cst[:, C_IDF:C_IDF + 128]
    ones_f = cst[:, C_ONES:C_ONES + 128]
    tri_f = cst[:, C_TRI:C_TRI + 128]
    par = cst[:, C_PAR:C_PAR + NPAR]
    idb = cstb[:, B_ID:B_ID + 128]
    mean_b = cstb[:, B_MEAN:B_MEAN + 128]
    negm = cstb[:, B_NEG:B_NEG + 512]
    eps_c = par[:, P_EPS:P_EPS + 1]
    one_c = par[:, P_ONE:P_ONE + 1]
    rstd_b = V(OFF_T, [128, TH], F32)
    sqb = V(OFF_T + 4128, [128, 2, TH], BF16)
    stmp = V(OFF_T + 8256, [128, 3, 512], F32)
    gtmp = V(OFF_T + 14400, [128, 2, 512], F32)

    P.add("sp", lambda e: e.dma_start(out=cst, in_=cst_d), w=["cst"], dma=True, sem="cst")
    P.add("pool", lambda e: e.dma_start(out=cstb, in_=cstb_d), w=["cstb"], dma=True, sem="cstb")
    P.add("pool", lambda e: e.dma_start(out=wdt, in_=wdt_d.rearrange("p (a b) -> p a b", a=16)), w=["wdt"], dma=True, sem="wdt")

    ring = {"n": 0, "ns": NSLOT}

    def wload(src_ap, shape):
        s = ring["n"] % ring["ns"]
        ring["n"] += 1
        v = V(OFF_RING + s * SLOT, shape, BF16)
        P.add("pool", lambda e: e.dma_start(out=v, in_=src_ap), w=[("ring", s)], dma=True, sem=("ring", s))
        return v, ("ring", s)

    def win_cols(w, c0, ncol=256):
        return w.rearrange("(kt p) f -> p kt f", p=128)[:, :, c0:c0 + ncol]

    xst = V(OFF_HID, [128, 4, D], F32)
    tcount = [0]
    for i in range(9):
        rows = 128 if i < 8 else HALO
        sb = i % 4
        P.add("sp", lambda e, i=i, rows=rows, sb=sb: e.dma_start(out=xst[0:rows, sb, :], in_=x_d[i * 128:i * 128 + rows, :]),
              w=[("xst", sb)], dma=True, sem=("xst", sb))
        for q in range(4):
            bk = tcount[0] % 6
            tcount[0] += 1
            for m in range(4):
                f = 4 * q + m
                P.add("pe", lambda e, bk=bk, m=m, f=f, rows=rows, sb=sb: e.transpose(
                    ps[:, bk, m * 128:m * 128 + rows], xst[0:rows, sb, f * 128:(f + 1) * 128], idf[0:rows, 0:rows]),
                    r=[("xst", sb), "cst"], w=[("ps", bk)])
            eng = "act" if (tcount[0] % 2) else "dve"
            src = lambda bk=bk, rows=rows: ps[:, bk, :].rearrange("p (m c) -> p m c", m=4)[:, :, 0:rows]
            dst = lambda q=q, i=i, rows=rows: h_fm[:, 4 * q:4 * q + 4, i * 128:i * 128 + rows]
            if eng == "act":
                P.add("act", lambda e, src=src, dst=dst: e.activation(out=dst(), in_=src(), func=AF.Copy),
                      r=[("ps", bk)], w=[("h", 4 * q + m) for m in range(4)])
            else:
                P.add("dve", lambda e, src=src, dst=dst: e.tensor_copy(dst(), src()),
                      r=[("ps", bk)], w=[("h", 4 * q + m) for m in range(4)])

    def rmsnorm(wi, tiles, out_fn=None, w_keys=("u",)):
        for ti, (c0, n) in enumerate(tiles):
            bk = 6 + ti % 2
            for f in range(DT):
                sbuf = f % 2
                P.add("act", lambda e, f=f, c0=c0, n=n, sbuf=sbuf: e.activation(
                    out=sqb[:, sbuf, 0:n], in_=h_fm[:, f, c0:c0 + n], func=AF.Square),
                    r=[("h", f)], w=[("sq", sbuf)])
                P.add("pe", lambda e, f=f, n=n, sbuf=sbuf, bk=bk: e.matmul(
                    ps[:, bk, 0:n], lhsT=mean_b, rhs=sqb[:, sbuf, 0:n], start=(f == 0), stop=(f == DT - 1)),
                    r=[("sq", sbuf), "cstb"], w=[("ps", bk)])
            P.add("act", lambda e, c0=c0, n=n, bk=bk: e.activation(
                out=rstd_b[:, c0:c0 + n], in_=ps[:, bk, 0:n], func=AF.Sqrt, bias=eps_c, scale=1.0),
                r=[("ps", bk), "cst"], w=[("rstd", ti)])
            P.add("dve", lambda e, c0=c0, n=n: e.reciprocal(rstd_b[:, c0:c0 + n], rstd_b[:, c0:c0 + n]),
                  r=[("rstd", ti)], w=[("rstd", ti)])
            for f in range(DT):
                o_ap = (lambda f=f, c0=c0, n=n: u_fm[:, f, c0:c0 + n]) if out_fn is None else (lambda f=f, c0=c0, n=n: out_fn(f, c0, n))
                wk = list(w_keys) if out_fn is None else [("h", f)]
                P.add("dve", lambda e, f=f, c0=c0, n=n, o_ap=o_ap: e.scalar_tensor_tensor(
                    out=o_ap(), in0=h_fm[:, f, c0:c0 + n], scalar=par[:, P_NW + wi * 16 + f:P_NW + wi * 16 + f + 1],
                    in1=rstd_b[:, c0:c0 + n], op0=ALU.mult, op1=ALU.mult),
                    r=[("h", f), ("rstd", ti), "cst"], w=wk)

    def ffn(w_in, w_out, tiles):
        blocks = [(b * 8, min(8, FT - b * 8)) for b in range((FT + 7) // 8)]
        unit = [0]
        oacc = [0]

        def emit_in(bi):
            j0, kb = blocks[bi]
            buf = bi % 2
            for pr in range(kb // 2):
                wg, rg = wload(win_cols(w_in, (j0 + 2 * pr) * 128), [128, 16, 256])
                wu, ru = wload(win_cols(w_in, DFF + (j0 + 2 * pr) * 128), [128, 16, 256])
                for jj2 in range(2):
                    jj = 2 * pr + jj2
                    for ti, (c0, n) in enumerate(tiles):
                        q = unit[0] % 3
                        unit[0] += 1
                        gb, ub = 2 * q, 2 * q + 1
                        for k in range(16):
                            P.add("pe", lambda e, k=k, wg=wg, jj2=jj2, c0=c0, n=n, gb=gb: e.matmul(
                                ps[:, gb, 0:n], lhsT=wg[:, k, jj2 * 128:(jj2 + 1) * 128], rhs=u_fm[:, k, c0:c0 + n],
                                start=(k == 0), stop=(k == 15)), r=[rg, "u"], w=[("ps", gb)])
                        for k in range(16):
                            P.add("pe", lambda e, k=k, wu=wu, jj2=jj2, c0=c0, n=n, ub=ub: e.matmul(
                                ps[:, ub, 0:n], lhsT=wu[:, k, jj2 * 128:(jj2 + 1) * 128], rhs=u_fm[:, k, c0:c0 + n],
                                start=(k == 0), stop=(k == 15)), r=[ru, "u"], w=[("ps", ub)])
                        P.add("act", lambda e, q=q, n=n, gb=gb: e.activation(out=stmp[:, q, 0:n], in_=ps[:, gb, 0:n], func=AF.Silu),
                              r=[("ps", gb)], w=[("stmp", q)])
                        P.add("dve", lambda e, q=q, n=n, ub=ub, buf=buf, jj=jj, c0=c0: e.tensor_tensor(
                            out=hid[:, buf * 8 + jj, c0:c0 + n], in0=stmp[:, q, 0:n], in1=ps[:, ub, 0:n], op=ALU.mult),
                            r=[("stmp", q), ("ps", ub)], w=[("hid", buf, jj)])

        def emit_out(bi):
            j0, kb = blocks[bi]
            buf = bi % 2
            wos = []
            for pr in range(kb // 2):
                src = w_out.rearrange("(kt p) f -> p kt f", p=128)[:, j0 + 2 * pr:j0 + 2 * pr + 2, :]
                wos.append(wload(src, [128, 2, D]))
            for f in range(DT):
                for ti, (c0, n) in enumerate(tiles):
                    ob = 6 + oacc[0] % 2
                    oacc[0] += 1
                    for kk in range(kb):
                        wv, rk = wos[kk // 2]
                        P.add("pe", lambda e, wv=wv, kk=kk, f=f, c0=c0, n=n, ob=ob, buf=buf, kb=kb: e.matmul(
                            ps[:, ob, 0:n], lhsT=wv[:, kk % 2, f * 128:(f + 1) * 128], rhs=hid[:, buf * 8 + kk, c0:c0 + n],
                            start=(kk == 0), stop=(kk == kb - 1)), r=[rk, ("hid", buf, kk)], w=[("ps", ob)])
                    P.add("dve", lambda e, f=f, c0=c0, n=n, ob=ob: e.scalar_tensor_tensor(
                        out=h_fm[:, f, c0:c0 + n], in0=ps[:, ob, 0:n], scalar=0.5, in1=h_fm[:, f, c0:c0 + n],
                        op0=ALU.mult, op1=ALU.add), r=[("ps", ob), ("h", f)], w=[("h", f)])

        nb = len(blocks)
        emit_in(0)
        for bi in range(nb):
            if bi + 1 < nb:
                emit_in(bi + 1)
            emit_out(bi)


    def bc_mid(ap2, n):
        return ap2.unsqueeze(1).to_broadcast([128, n, ap2.shape[1]])

    def bc_last(ap2, n):
        return ap2.unsqueeze(2).to_broadcast([128, ap2.shape[1], n])

    def mixer():
        X_tm = V(OFF_H, [128, 8, 2048], BF16)
        zs_tm = V(OFF_H + 32768, [128, 8, 2048], BF16)
        yb_fm = V(OFF_U, [128, DT, T], BF16)
        ya_fm = V(OFF_HID, [128, DT, T], BF16)
        B_fm = V(OFF_R2, [128, 4, T], BF16)
        C_fm = V(OFF_R2 + 8192, [128, 4, T], BF16)
        st_bf = V(OFF_R2 + 16384, [128, 2048], BF16)
        dtv = V(OFF_R2 + 20480, [128, 8, 32], F32)
        A_tm = V(OFF_R2 + 21504, [128, 8, 32], F32)
        acs = V(OFF_R2 + 22528, [128, 8, 32], F32)
        tot = V(OFF_R2 + 23552, [128, 8, 32], F32)
        sm = lambda i: V(OFF_T + 1024 * i, [128, 8, 32], F32)
        nacs, eacs, wd, cd, wseg, suf, tmpa, tmpb = [sm(i) for i in range(8)]
        a_b = V(OFF_T + 8192, [128, 32], F32)
        totseg = V(OFF_T + 8320, [128, 32], F32)
        ss4 = V(OFF_T + 8448, [128, 8], F32)
        rs4 = V(OFF_T + 8480, [128, 8], F32)
        cfac = V(OFF_T + 8512, [128, 32], F32)
        A = OFF_HID
        pcb = V(A, [128, TH], F32)
        cvb = V(A + 4128, [128, T], F32)
        xsb = V(A + 8224, [128, 2, T], BF16)
        csb = V(A + 8224, [128, TH], F32)
        yast = V(A + 12352, [128, 2, T], BF16)
        btmA = V(A + 16448, [128, 2, 512], BF16)
        xwA = V(A + 18496, [128, 2, 512], BF16)
        sA = V(A + 20544, [128, 2048], F32)
        dsk = par[:, P_DSK:P_DSK + 32]

        ring["ns"] = 3
        rmsnorm(1, T3)
        P.add("sp", lambda e: e.dma_start(out=h_sp.rearrange("p (f t) -> p f t", f=DT), in_=h_fm[:, :, 0:T]),
              r=[("h", f) for f in range(DT)], w=["hsp"], dma=True, sem="hsp")

        cnt = [0]
        xcnt = [0]

        def conv_tile(src_cols, wbase, ntap, j_w):
            off0 = 8 - (ntap - 1)
            P.add("dve", lambda e: e.tensor_scalar(out=cvb, in0=pcb[:, off0:off0 + T], scalar1=par[:, wbase + j_w * ntap:wbase + j_w * ntap + 1],
                                                   scalar2=None, op0=ALU.mult), r=["pcb", "cst"], w=["cvb"])
            for k in range(1, ntap):
                P.add("dve", lambda e, k=k: e.scalar_tensor_tensor(out=cvb, in0=pcb[:, off0 + k:off0 + k + T],
                                                                   scalar=par[:, wbase + j_w * ntap + k:wbase + j_w * ntap + k + 1], in1=cvb, op0=ALU.mult, op1=ALU.add),
                      r=["pcb", "cvb", "cst"], w=["cvb"])

        def proj3(wv, rk, jj2):
            s3 = cnt[0] % 2
            cnt[0] += 1
            for ti, (c0, n) in enumerate(T3):
                for k in range(16):
                    P.add("pe", lambda e, k=k, ti=ti, c0=c0, n=n, s3=s3: e.matmul(
                        ps[:, 3 * s3 + ti, 0:n], lhsT=wv[:, k, jj2 * 128:(jj2 + 1) * 128], rhs=u_fm[:, k, c0:c0 + n],
                        start=(k == 0), stop=(k == 15)), r=[rk, "u"], w=[("ps", 3 * s3 + ti)])
            return s3

        def evac3(s3, dst, key):
            rr = [("ps", 3 * s3 + t) for t in range(3)]
            P.add("act", lambda e: e.activation(out=dst[:, 8:696].rearrange("p (a b) -> p a b", a=2), in_=ps[:, 3 * s3:3 * s3 + 2, 0:344], func=AF.Copy), r=rr, w=[key])
            P.add("act", lambda e: e.activation(out=dst[:, 696:1032], in_=ps[:, 3 * s3 + 2, 0:336], func=AF.Copy), r=rr, w=[key])
            P.add("act", lambda e: e.activation(out=dst[:, 0:8], in_=ps[:, 3 * s3 + 2, 336:344], func=AF.Copy), r=rr, w=[key])

        jorder = list(range(16, 24)) + list(range(0, 16))
        for pi in range(12):
            j0 = jorder[2 * pi]
            if j0 == 0:
                P.barrier(("act", "dve"))
            wv, rk = wload(win_cols(wmi, 8192 + j0 * 128), [128, 16, 256])
            for jj2 in range(2):
                j = j0 + jj2
                s3 = proj3(wv, rk, jj2)
                evac3(s3, pcb, "pcb")
                conv_tile(None, P_SSDW, 4, j)
                bias = par[:, P_SSDB + j:P_SSDB + j + 1]
                if j >= 20:
                    P.add("act", lambda e, j=j, bias=bias: e.activation(out=C_fm[:, j - 20, :], in_=cvb, func=AF.Silu, bias=bias), r=["cvb", "cst"], w=[("C", j - 20)])
                elif j >= 16:
                    P.add("act", lambda e, j=j, bias=bias: e.activation(out=B_fm[:, j - 16, :], in_=cvb, func=AF.Silu, bias=bias), r=["cvb", "cst"], w=[("B", j - 16)])
                else:
                    xb = xcnt[0] % 2
                    xcnt[0] += 1
                    P.add("act", lambda e, xb=xb, bias=bias: e.activation(out=xsb[:, xb, :], in_=cvb, func=AF.Silu, bias=bias), r=["cvb", "cst"], w=[("xsb", xb)])
                    bk = 6 + xb
                    for i in range(8):
                        P.add("pe", lambda e, i=i, xb=xb, bk=bk: e.transpose(psb(bk)[:, i * 128:(i + 1) * 128], xsb[:, xb, i * 128:(i + 1) * 128], idb),
                              r=[("xsb", xb), "cstb"], w=[("ps", bk)])
                    P.add("dve", lambda e, j=j, bk=bk: e.tensor_copy(X_tm[:, :, j * 128:(j + 1) * 128], psb(bk)[:, 0:1024].rearrange("p (i c) -> p i c", i=8)),
                          r=[("ps", bk)], w=[("X", j // 4)])

        for i in range(8):
            for k in range(16):
                P.add("pe", lambda e, i=i, k=k: e.matmul(ps[:, 7, i * 32:(i + 1) * 32], lhsT=u_fm[:, k, i * 128:(i + 1) * 128], rhs=wdt[:, k, :],
                                                         start=(k == 0), stop=(k == 15)), r=["u", "wdt"], w=[("ps", 7)])
        f256 = lambda v: v.rearrange("p a b -> p (a b)")
        P.add("dve", lambda e: e.tensor_tensor(out=dtv, in0=ps[:, 7, 0:256].rearrange("p (a b) -> p a b", a=8), in1=bc_mid(par[:, P_DTB:P_DTB + 32], 8), op=ALU.add),
              r=[("ps", 7), "cst"], w=["dtv"])
        P.add("act", lambda e: e.activation(out=f256(tmpa), in_=f256(dtv), func=AF.Abs), r=["dtv"], w=["tmpa"])
        P.add("act", lambda e: e.activation(out=f256(tmpa), in_=f256(tmpa), func=AF.Exp, scale=-1.0), r=["tmpa"], w=["tmpa"])
        P.add("act", lambda e: e.activation(out=f256(tmpa), in_=f256(tmpa), func=AF.Ln, bias=one_c, scale=1.0), r=["tmpa", "cst"], w=["tmpa"])
        P.add("dve", lambda e: e.tensor_scalar_max(out=f256(dtv), in0=f256(dtv), scalar1=0.0), r=["dtv"], w=["dtv"])
        P.add("dve", lambda e: e.tensor_tensor(out=f256(dtv), in0=f256(dtv), in1=f256(tmpa), op=ALU.add), r=["dtv", "tmpa"], w=["dtv"])
        P.add("act", lambda e: e.activation(out=a_b, in_=par[:, P_ALOG:P_ALOG + 32], func=AF.Exp), r=["cst"], w=["a_b"])
        P.add("dve", lambda e: e.tensor_scalar(out=a_b, in0=a_b, scalar1=-1.0, scalar2=None, op0=ALU.mult), r=["a_b"], w=["a_b"])
        P.add("dve", lambda e: e.tensor_tensor(out=A_tm, in0=dtv, in1=bc_mid(a_b, 8), op=ALU.mult), r=["dtv", "a_b"], w=["A_tm"])
        for c in range(8):
            P.add("pe", lambda e, c=c: e.matmul(ps[:, 6, c * 32:(c + 1) * 32], lhsT=tri_f, rhs=A_tm[:, c, :], start=True, stop=True), r=["A_tm", "cst"], w=[("ps", 6)])
        for c in range(8):
            P.add("pe", lambda e, c=c: e.matmul(ps[:, 6, 256 + c * 32:256 + (c + 1) * 32], lhsT=ones_f, rhs=A_tm[:, c, :], start=True, stop=True), r=["A_tm", "cst"], w=[("ps", 6)])
        P.add("act", lambda e: e.activation(out=f256(acs), in_=ps[:, 6, 0:256], func=AF.Copy), r=[("ps", 6)], w=["acs"])
        P.add("act", lambda e: e.activation(out=f256(tot), in_=ps[:, 6, 256:512], func=AF.Copy), r=[("ps", 6)], w=["tot"])
        P.add("dve", lambda e: e.tensor_scalar(out=f256(nacs), in0=f256(acs), scalar1=-1.0, scalar2=None, op0=ALU.mult), r=["acs"], w=["nacs"])
        P.add("act", lambda e: e.activation(out=f256(eacs), in_=f256(acs), func=AF.Exp), r=["acs"], w=["eacs"])
        P.add("act", lambda e: e.activation(out=f256(cd), in_=f256(tot), func=AF.Exp), r=["tot"], w=["cd"])
        P.add("dve", lambda e: e.tensor_tensor(out=f256(tmpb), in0=f256(tot), in1=f256(acs), op=ALU.subtract), r=["tot", "acs"], w=["tmpb"])
        P.add("act", lambda e: e.activation(out=f256(wd), in_=f256(tmpb), func=AF.Exp), r=["tmpb"], w=["wd"])
        P.add("dve", lambda e: e.tensor_tensor(out=f256(wd), in0=f256(wd), in1=f256(dtv), op=ALU.mult), r=["wd", "dtv"], w=["wd"])
        P.add("dve", lambda e: e.memset(suf[:, 7, :], 0.0), w=["suf"])
        for c in range(6, -1, -1):
            P.add("dve", lambda e, c=c: e.tensor_tensor(out=suf[:, c, :], in0=suf[:, c + 1, :], in1=tot[:, c + 1, :], op=ALU.add), r=["suf", "tot"], w=["suf"])
        P.add("dve", lambda e: e.tensor_tensor(out=totseg, in0=suf[:, 0, :], in1=tot[:, 0, :], op=ALU.add), r=["suf", "tot"], w=["totseg"])
        P.add("dve", lambda e: e.tensor_tensor(out=f256(tmpb), in0=f256(tmpb), in1=f256(suf), op=ALU.add), r=["tmpb", "suf", "wd"], w=["tmpb"])
        P.add("act", lambda e: e.activation(out=f256(wseg), in_=f256(tmpb), func=AF.Exp), r=["tmpb"], w=["wseg"])
        P.add("dve", lambda e: e.tensor_tensor(out=f256(wseg), in0=f256(wseg), in1=f256(dtv), op=ALU.mult), r=["wseg", "dtv"], w=["wseg"])

        def btm_chunk(c, dstv, key):
            for g in range(4):
                P.add("pe", lambda e, g=g, c=c: e.transpose(psb(7)[:, g * 128:(g + 1) * 128], B_fm[:, g, c * 128:(c + 1) * 128], idb),
                      r=[("B", g), "cstb"], w=[("ps", 7)])
            P.add("act", lambda e: e.activation(out=dstv, in_=psb(7)[:, 0:512], func=AF.Copy), r=[("ps", 7)], w=[key])

        for c in range(8):
            bb = c % 2
            btm_chunk(c, btmA[:, bb, :], ("btmA", bb))
            for g in range(4):
                xb = (c * 4 + g) % 2
                P.add("dve", lambda e, c=c, g=g, xb=xb: e.tensor_tensor(
                    out=xwA[:, xb, :].rearrange("p (h d) -> p h d", h=8), in0=X_tm[:, c, g * 512:(g + 1) * 512].rearrange("p (h d) -> p h d", h=8),
                    in1=bc_last(wseg[:, c, 8 * g:8 * g + 8], 64), op=ALU.mult), r=[("X", g), "wseg"], w=[("xwA", xb)])
                P.add("pe", lambda e, c=c, g=g, xb=xb, bb=bb: e.matmul(ps[:, g, :], lhsT=btmA[:, bb, g * 128:(g + 1) * 128], rhs=xwA[:, xb, :],
                                                                       start=(c == 0), stop=(c == 7)), r=[("btmA", bb), ("xwA", xb)], w=[("ps", g)])
        for g in range(4):
            P.add("act" if g % 2 else "dve",
                  (lambda e, g=g: e.activation(out=sA[:, g * 512:(g + 1) * 512], in_=ps[:, g, :], func=AF.Copy)) if g % 2 else
                  (lambda e, g=g: e.tensor_copy(sA[:, g * 512:(g + 1) * 512], ps[:, g, :])), r=[("ps", g)], w=[("sA", g)])
        cin = cc_in.ap()
        P.add("sp", lambda e: e.dma_start(out=cin[:, 0:2048], in_=sA), r=[("sA", g) for g in range(4)], w=["cin0"], dma=True, sem="cin0")
        P.add("sp", lambda e: e.dma_start(out=cin[:, 2048:2080], in_=totseg), r=["totseg"], w=["cin1"], dma=True, sem="cin1")
        P.add("pool", lambda e: e.collective_compute("AllGather", ALU.bypass, replica_groups=[list(range(ncores))],
                                                     ins=[cc_in.ap().opt()], outs=[cc_out.ap().opt()]),
              r=["cin0", "cin1"], w=["cout"], dma=True, sem="cc", inc=1)
        P.barrier(("pe", "act", "dve"))

        zc = [0]
        for cb in range(8):
            wv, rk = wload(win_cols(wmi, 6144 + cb * 256), [128, 16, 256])
            for i in range(8):
                bk = zc[0] % 8
                zc[0] += 1
                for k in range(16):
                    P.add("pe", lambda e, k=k, i=i, bk=bk, wv=wv: e.matmul(ps[:, bk, 0:256], lhsT=u_fm[:, k, i * 128:(i + 1) * 128], rhs=wv[:, k, :],
                                                                         start=(k == 0), stop=(k == 15)), r=[rk, "u"], w=[("ps", bk)])
                P.add("act", lambda e, i=i, cb=cb, bk=bk: e.activation(out=zs_tm[:, i, cb * 256:(cb + 1) * 256], in_=ps[:, bk, 0:256], func=AF.Silu),
                      r=[("ps", bk)], w=[("zs", cb // 2)])

        yac = [0]
        for pi in range(8):
            j0 = 2 * pi
            wc, rc = wload(win_cols(wmi, 2048 + j0 * 128), [128, 16, 256])
            wx, rx = wload(win_cols(wmi, 4096 + j0 * 128), [128, 16, 256])
            wb, rb = wload(win_cols(wmi, j0 * 128), [128, 16, 256])
            for jj2 in range(2):
                j = j0 + jj2
                s3 = proj3(wc, rc, jj2)
                evac3(s3, csb, "csb")
                s3 = proj3(wx, rx, jj2)
                rr = [("ps", 3 * s3 + t) for t in range(3)]
                P.add("dve", lambda e, s3=s3: e.tensor_tensor(out=pcb[:, 8:696].rearrange("p (a b) -> p a b", a=2), in0=csb[:, 8:696].rearrange("p (a b) -> p a b", a=2),
                                                              in1=ps[:, 3 * s3:3 * s3 + 2, 0:344], op=ALU.mult), r=rr + ["csb"], w=["pcb"])
                P.add("dve", lambda e, s3=s3: e.tensor_tensor(out=pcb[:, 696:1032], in0=csb[:, 696:1032], in1=ps[:, 3 * s3 + 2, 0:336], op=ALU.mult), r=rr + ["csb"], w=["pcb"])
                P.add("dve", lambda e, s3=s3: e.tensor_tensor(out=pcb[:, 0:8], in0=csb[:, 0:8], in1=ps[:, 3 * s3 + 2, 336:344], op=ALU.mult), r=rr + ["csb"], w=["pcb"])
                conv_tile(None, P_SCW, 3, j)
                s3 = proj3(wb, rb, jj2)
                rr = [("ps", 3 * s3 + t) for t in range(3)]
                yb = yac[0] % 2
                yac[0] += 1
                P.add("dve", lambda e, s3=s3, yb=yb: e.tensor_tensor(out=yast[:, yb, 0:688].rearrange("p (a b) -> p a b", a=2), in0=cvb[:, 0:688].rearrange("p (a b) -> p a b", a=2),
                                                                     in1=ps[:, 3 * s3:3 * s3 + 2, 0:344], op=ALU.mult), r=rr + ["cvb"], w=[("yast", yb)])
                P.add("dve", lambda e, s3=s3, yb=yb: e.tensor_tensor(out=yast[:, yb, 688:1024], in0=cvb[:, 688:1024], in1=ps[:, 3 * s3 + 2, 0:336], op=ALU.mult),
                      r=rr + ["cvb"], w=[("yast", yb)])
                P.add("sp", lambda e, j=j, yb=yb: e.dma_start(out=ya_sp[:, j * T:(j + 1) * T], in_=yast[:, yb, :]), r=[("yast", yb)], w=[("yasp", j)],
                      dma=True, sem=("yast", yb))
        P.barrier(("pe", "act", "dve"))

        state = V(A, [128, 2048], F32)
        xd = V(A + 8192, [128, 2, 512], BF16)
        xdt = V(A + 10240, [128, 2, 512], BF16)
        btmB = V(A + 12288, [128, 2, 512], BF16)
        am = V(A + 14336, [128, 2, 512], F32)
        lt = V(A + 18432, [128, 2, 512], F32)
        mt = V(A + 22528, [128, 2, 512], BF16)
        cbt = V(A + 24576, [128, 512], F32)
        t1 = V(A + 26624, [128, 2, 512], F32)
        ybg = V(A + 30720, [128, 2, 512], BF16)
        lr = V(OFF_U, [128, 3, 2080], F32)
        cout = cc_out.ap()
        P.add("dve", lambda e: e.memset(state, 0.0), w=["state"])
        for n_, r_ in enumerate((0, 1, 2, 4, 5, 6)):
            lb = n_ % 3
            P.add("sp", lambda e, r_=r_, lb=lb: e.dma_start(out=lr[:, lb, :], in_=cout[r_ * 128:(r_ + 1) * 128, :]), r=["cout"], w=[("lr", lb)], dma=True, sem=("lr", lb))
            m_r = par[:, P_SEG + r_:P_SEG + r_ + 1]
            P.add("act", lambda e, lb=lb: e.activation(out=cfac, in_=lr[:, lb, 2048:2080], func=AF.Exp), r=[("lr", lb)], w=["cfac"])
            P.add("dve", lambda e: e.tensor_scalar(out=cfac, in0=cfac, scalar1=-1.0, scalar2=None, op0=ALU.add), r=["cfac"], w=["cfac"])
            P.add("dve", lambda e, m_r=m_r: e.tensor_scalar(out=cfac, in0=cfac, scalar1=m_r, scalar2=None, op0=ALU.mult), r=["cfac", "cst"], w=["cfac"])
            P.add("dve", lambda e: e.tensor_scalar(out=cfac, in0=cfac, scalar1=1.0, scalar2=None, op0=ALU.add), r=["cfac"], w=["cfac"])
            P.add("dve", lambda e: e.tensor_tensor(out=state.rearrange("p (h d) -> p h d", h=32), in0=state.rearrange("p (h d) -> p h d", h=32),
                                                   in1=bc_last(cfac, 64), op=ALU.mult), r=["state", "cfac"], w=["state"])
            P.add("dve", lambda e, lb=lb, m_r=m_r: e.scalar_tensor_tensor(out=state, in0=lr[:, lb, 0:2048], scalar=m_r, in1=state, op0=ALU.mult, op1=ALU.add),
                  r=[("lr", lb), "state", "cst"], w=["state"])
        for g in range(4):
            P.add("act", lambda e, g=g: e.activation(out=st_bf[:, g * 512:(g + 1) * 512], in_=state[:, g * 512:(g + 1) * 512], func=AF.Copy), r=["state"], w=[("sbf", g)])
        P.barrier(("pe", "act", "dve"))

        qc = [0]
        for c in range(8):
            cs = slice(c * 128, (c + 1) * 128)
            bb = c % 2
            btm_chunk(c, btmB[:, bb, :], ("btmB", bb))
            for g in range(4):
                P.add("pe", lambda e, g=g, cs=cs: e.matmul(ps[:, 0, g * 128:(g + 1) * 128], lhsT=B_fm[:, g, cs], rhs=C_fm[:, g, cs], start=True, stop=True),
                      r=[("B", g), ("C", g)], w=[("ps", 0)])
            P.add("act", lambda e: e.activation(out=cbt, in_=ps[:, 0, :], func=AF.Copy), r=[("ps", 0)], w=["cbt"])
            for g in range(4):
                gb = g % 2
                Xg = lambda c=c, g=g: X_tm[:, c, g * 512:(g + 1) * 512].rearrange("p (h d) -> p h d", h=8)
                P.add("dve", lambda e, c=c, g=g, gb=gb, Xg=Xg: e.tensor_tensor(out=xdt[:, gb, :].rearrange("p (h d) -> p h d", h=8), in0=Xg(),
                                                                               in1=bc_last(dtv[:, c, 8 * g:8 * g + 8], 64), op=ALU.mult), r=[("X", g), "dtv"], w=[("xdt", gb)])
                P.add("dve", lambda e, c=c, g=g, gb=gb, Xg=Xg: e.tensor_tensor(out=xd[:, gb, :].rearrange("p (h d) -> p h d", h=8), in0=Xg(),
                                                                              in1=bc_last(wd[:, c, 8 * g:8 * g + 8], 64), op=ALU.mult), r=[("X", g), "wd"], w=[("xd", gb)])
                ydb = 3 + gb
                for q2 in range(2):
                    q = 2 * g + q2
                    qb = qc[0] % 2
                    qc[0] += 1
                    P.add("dve", lambda e, c=c, q=q, qb=qb: e.tensor_tensor(out=am[:, qb, :].rearrange("p (h l) -> p h l", h=4), in0=bc_mid(tri_f, 4),
                                                                           in1=bc_last(A_tm[:, c, 4 * q:4 * q + 4], 128), op=ALU.mult), r=["A_tm", "cst"], w=[("am", qb)])
                    P.add("pe", lambda e, qb=qb: e.matmul(ps[:, 1 + qb, :], lhsT=ones_f, rhs=am[:, qb, :], start=True, stop=False), r=[("am", qb), "cst"], w=[("ps", 1 + qb)])
                    P.add("pe", lambda e, qb=qb: e.matmul(ps[:, 1 + qb, :], lhsT=idb, rhs=negm, start=False, stop=True), r=["cstb"], w=[("ps", 1 + qb)])
                    for hh in range(4):
                        P.add("act", lambda e, c=c, q=q, qb=qb, hh=hh: e.activation(out=lt[:, qb, hh * 128:(hh + 1) * 128], in_=ps[:, 1 + qb, hh * 128:(hh + 1) * 128],
                                                                                  func=AF.Exp, bias=nacs[:, c, 4 * q + hh:4 * q + hh + 1], scale=1.0),
                              r=[("ps", 1 + qb), "nacs"], w=[("lt", qb)])
                    P.add("dve", lambda e, qb=qb, g=g: e.tensor_tensor(out=mt[:, qb, :].rearrange("p (h l) -> p h l", h=4), in0=lt[:, qb, :].rearrange("p (h l) -> p h l", h=4),
                                                                      in1=bc_mid(cbt[:, g * 128:(g + 1) * 128], 4), op=ALU.mult), r=[("lt", qb), "cbt"], w=[("mt", qb)])
                    for hh in range(4):
                        hl = 4 * q2 + hh
                        P.add("pe", lambda e, qb=qb, hh=hh, hl=hl, gb=gb, ydb=ydb: e.matmul(ps[:, ydb, hl * 64:(hl + 1) * 64], lhsT=mt[:, qb, hh * 128:(hh + 1) * 128],
                                                                                         rhs=xdt[:, gb, hl * 64:(hl + 1) * 64], start=True, stop=True),
                              r=[("mt", qb), ("xdt", gb)], w=[("ps", ydb)])
                P.add("pe", lambda e, g=g, cs=cs: e.matmul(ps[:, 5, :], lhsT=C_fm[:, g, cs], rhs=st_bf[:, g * 512:(g + 1) * 512], start=True, stop=True),
                      r=[("C", g), ("sbf", g)], w=[("ps", 5)])
                tb = g % 2
                t3v = lambda tb=tb: t1[:, tb, :].rearrange("p (h d) -> p h d", h=8)
                P.add("dve", lambda e, c=c, g=g, t3v=t3v: e.tensor_tensor(out=t3v(), in0=ps[:, 5, :].rearrange("p (h d) -> p h d", h=8),
                                                                         in1=bc_last(eacs[:, c, 8 * g:8 * g + 8], 64), op=ALU.mult), r=[("ps", 5), "eacs"], w=[("t1", tb)])
                P.add("dve", lambda e, tb=tb, ydb=ydb: e.tensor_tensor(out=t1[:, tb, :], in0=t1[:, tb, :], in1=ps[:, ydb, :], op=ALU.add), r=[("t1", tb), ("ps", ydb)], w=[("t1", tb)])
                P.add("dve", lambda e, g=g, Xg=Xg: e.tensor_tensor(out=lt[:, 0, :].rearrange("p (h d) -> p h d", h=8), in0=Xg(), in1=bc_last(dsk[:, 8 * g:8 * g + 8], 64), op=ALU.mult),
                      r=[("X", g), "cst", ("lt", 0)], w=[("lt", 0)])
                P.add("dve", lambda e, tb=tb: e.tensor_tensor(out=t1[:, tb, :], in0=t1[:, tb, :], in1=lt[:, 0, :], op=ALU.add), r=[("t1", tb), ("lt", 0)], w=[("t1", tb)])
                P.add("dve", lambda e, c=c, g=g, tb=tb: e.tensor_tensor(out=t1[:, tb, :], in0=t1[:, tb, :], in1=zs_tm[:, c, g * 512:(g + 1) * 512], op=ALU.mult),
                      r=[("t1", tb), ("zs", g)], w=[("t1", tb)])
                P.add("dve", lambda e, g=g: e.memset(ss4[:, g:g + 1], 0.0), r=[("ss", g)], w=[("ss", g)])
                P.add("act", lambda e, tb=tb, g=g: e.activation(out=lt[:, 1, :], in_=t1[:, tb, :], func=AF.Square, accum_out=ss4[:, g:g + 1]),
                      r=[("t1", tb), ("lt", 1)], w=[("ss", g), ("lt", 1)])
                P.add("act", lambda e, g=g: e.activation(out=rs4[:, g:g + 1], in_=ss4[:, g:g + 1], func=AF.Sqrt, bias=eps_c, scale=1.0 / 512), r=[("ss", g), "cst"], w=[("rs", g)])
                P.add("dve", lambda e, g=g: e.reciprocal(rs4[:, g:g + 1], rs4[:, g:g + 1]), r=[("rs", g)], w=[("rs", g)])
                P.add("dve", lambda e, g=g, tb=tb: e.tensor_scalar(out=ybg[:, tb, :], in0=t1[:, tb, :], scalar1=rs4[:, g:g + 1], scalar2=None, op0=ALU.mult),
                      r=[("t1", tb), ("rs", g)], w=[("ybg", tb)])
                for m in range(4):
                    P.add("pe", lambda e, m=m, tb=tb: e.transpose(psb(6)[:, m * 128:(m + 1) * 128], ybg[:, tb, m * 128:(m + 1) * 128], idb), r=[("ybg", tb), "cstb"], w=[("ps", 6)])
                P.add("dve", lambda e, g=g, cs=cs: e.tensor_tensor(out=yb_fm[:, 4 * g:4 * g + 4, cs], in0=psb(6)[:, 0:512].rearrange("p (m t) -> p m t", m=4),
                                                                  in1=bc_last(par[:, P_SSDN + 4 * g:P_SSDN + 4 * g + 4], 128), op=ALU.mult), r=[("ps", 6), "cst"], w=[("yb", g)])
                if c < 7:
                    P.add("pe", lambda e, g=g, gb=gb, bb=bb: e.matmul(ps[:, 7, :], lhsT=btmB[:, bb, g * 128:(g + 1) * 128], rhs=xd[:, gb, :], start=True, stop=True),
                          r=[("btmB", bb), ("xd", gb)], w=[("ps", 7)])
                    sg = lambda g=g: state[:, g * 512:(g + 1) * 512]
                    P.add("dve", lambda e, c=c, g=g, sg=sg: e.tensor_tensor(out=sg().rearrange("p (h d) -> p h d", h=8), in0=sg().rearrange("p (h d) -> p h d", h=8),
                                                                           in1=bc_last(cd[:, c, 8 * g:8 * g + 8], 64), op=ALU.mult), r=["cd", ("st", g)], w=[("st", g)])
                    P.add("dve", lambda e, sg=sg: e.tensor_tensor(out=sg(), in0=sg(), in1=ps[:, 7, :], op=ALU.add), r=[("st", g), ("ps", 7)], w=[("st", g)])
                    P.add("act", lambda e, g=g, sg=sg: e.activation(out=st_bf[:, g * 512:(g + 1) * 512], in_=sg(), func=AF.Copy), r=[("st", g)], w=[("sbf", g)])
        P.barrier(("pe", "act", "dve", "sp"))

        P.add("sp", lambda e: e.dma_start(out=h_fm[:, :, 0:T], in_=h_sp.rearrange("p (f t) -> p f t", f=DT)), r=["hsp"], w=[("h", f) for f in range(DT)], dma=True, sem="hrl")
        P.add("sp", lambda e: e.dma_start(out=ya_fm, in_=ya_sp.rearrange("p (f t) -> p f t", f=DT)), r=[("yasp", j) for j in range(16)], w=["ya"], dma=True, sem="yrl")
        oc = [0]
        for f in range(DT):
            wv, rk = wload(wmo.rearrange("(kt p) f -> p kt f", p=128)[:, :, f * 128:(f + 1) * 128], [128, 32, 128])
            for (c0, n) in T2:
                ob = oc[0] % 4
                oc[0] += 1
                for k in range(32):
                    rhs = (lambda k=k, c0=c0, n=n: ya_fm[:, k, c0:c0 + n]) if k < 16 else (lambda k=k, c0=c0, n=n: yb_fm[:, k - 16, c0:c0 + n])
                    P.add("pe", lambda e, k=k, wv=wv, rhs=rhs, ob=ob: e.matmul(ps[:, ob, :], lhsT=wv[:, k, :], rhs=rhs(), start=(k == 0), stop=(k == 31)),
                          r=[rk, "ya" if k < 16 else ("yb", (k - 16) // 4)], w=[("ps", ob)])
                P.add("dve", lambda e, f=f, c0=c0, n=n, ob=ob: e.tensor_tensor(out=h_fm[:, f, c0:c0 + n], in0=h_fm[:, f, c0:c0 + n], in1=ps[:, ob, :], op=ALU.add),
                      r=[("ps", ob), ("h", f)], w=[("h", f)])
        P.barrier(("pe", "act", "dve", "pool"))
        ring["ns"] = NSLOT

    def ple():
        P.barrier(("pe", "act", "dve", "sp", "pool"))
        rmsnorm(3, T2)
        pst = V(OFF_HID, [128, 8, 256], F32)
        p_fm = V(OFF_HID + 8192, [128, 2, T], BF16)
        P.add("sp", lambda e: e.dma_start(out=pst, in_=p_d.rearrange("(i p) c -> p i c", p=128)), w=["pst"], dma=True, sem="pst")
        for kt in range(2):
            for ih in range(2):
                bk = 4 + (2 * kt + ih) % 2
                for m in range(4):
                    i = 4 * ih + m
                    P.add("pe", lambda e, kt=kt, i=i, m=m, bk=bk: e.transpose(ps[:, bk, m * 128:(m + 1) * 128], pst[:, i, kt * 128:(kt + 1) * 128], idf),
                          r=["pst", "cst"], w=[("ps", bk)])
                P.add("act", lambda e, kt=kt, ih=ih, bk=bk: e.activation(out=p_fm[:, kt, ih * 512:(ih + 1) * 512], in_=ps[:, bk, :], func=AF.Copy), r=[("ps", bk)], w=["p_fm"])
        wp = V(OFF_HID + 12288, [128, 2, D], BF16)
        rp = "wp"
        P.add("pool", lambda e: e.dma_start(out=wp, in_=wpp.rearrange("(kt p) f -> p kt f", p=128)), w=["wp"], dma=True, sem="wp")
        uc = [0]
        for fp in range(8):
            wg, rg = wload(win_cols(wpg, fp * 256), [128, 16, 256])
            for f2 in range(2):
                f = 2 * fp + f2
                for (c0, n) in T2:
                    q = uc[0] % 2
                    uc[0] += 1
                    gb, pb = 2 * q, 2 * q + 1
                    for k in range(16):
                        P.add("pe", lambda e, k=k, wg=wg, f2=f2, c0=c0, n=n, gb=gb: e.matmul(ps[:, gb, 0:n], lhsT=wg[:, k, f2 * 128:(f2 + 1) * 128], rhs=u_fm[:, k, c0:c0 + n],
                                                                                         start=(k == 0), stop=(k == 15)), r=[rg, "u"], w=[("ps", gb)])
                    for k in range(2):
                        P.add("pe", lambda e, k=k, f=f, c0=c0, n=n, pb=pb: e.matmul(ps[:, pb, 0:n], lhsT=wp[:, k, f * 128:(f + 1) * 128], rhs=p_fm[:, k, c0:c0 + n],
                                                                                   start=(k == 0), stop=(k == 1)), r=[rp, "p_fm"], w=[("ps", pb)])
                    P.add("act", lambda e, q=q, n=n, gb=gb: e.activation(out=gtmp[:, q, 0:n], in_=ps[:, gb, 0:n], func=AF.Sigmoid), r=[("ps", gb)], w=[("gtmp", q)])
                    P.add("dve", lambda e, q=q, n=n, pb=pb: e.tensor_tensor(out=gtmp[:, q, 0:n], in0=gtmp[:, q, 0:n], in1=ps[:, pb, 0:n], op=ALU.mult),
                          r=[("gtmp", q), ("ps", pb)], w=[("gtmp", q)])
                    P.add("dve", lambda e, q=q, n=n, f=f, c0=c0: e.tensor_tensor(out=h_fm[:, f, c0:c0 + n], in0=h_fm[:, f, c0:c0 + n], in1=gtmp[:, q, 0:n], op=ALU.add),
                          r=[("gtmp", q), ("h", f)], w=[("h", f)])

    rmsnorm(0, T3)
    ffn(w1i, w1o, T3)

    if stage >= 2 and not skip_mixer:
        mixer()
    if stage >= 3:
        rmsnorm(2, T2)
        ffn(w2i, w2o, T2)
    if stage >= 4:
        ple()

    if stage >= 5:
        rmsnorm(4, T2, out_fn=lambda f, c0, n: h_fm[:, f, c0:c0 + n])
    P.barrier(("pe", "act", "dve", "sp"))
    ot = V(OFF_HID, [128, 2, D], F32)
    oc = [0]
    for i in range(8):
        ob = i % 2
        for q in range(4):
            bk = oc[0] % 6
            oc[0] += 1
            for m in range(4):
                f = 4 * q + m
                P.add("pe", lambda e, bk=bk, m=m, f=f, i=i: e.transpose(
                    ps[:, bk, m * 128:(m + 1) * 128], h_fm[:, f, i * 128:(i + 1) * 128], idf),
                    r=[("h", f), "cst"], w=[("ps", bk)])
            if oc[0] % 2:
                P.add("act", lambda e, bk=bk, ob=ob, q=q: e.activation(out=ot[:, ob, q * 512:(q + 1) * 512], in_=ps[:, bk, :], func=AF.Copy),
                      r=[("ps", bk)], w=[("ot", ob, q)])
            else:
                P.add("dve", lambda e, bk=bk, ob=ob, q=q: e.tensor_copy(ot[:, ob, q * 512:(q + 1) * 512], ps[:, bk, :]),
                      r=[("ps", bk)], w=[("ot", ob, q)])
        P.add("sp", lambda e, i=i, ob=ob: e.dma_start(out=out_d[i * 128:(i + 1) * 128, :], in_=ot[:, ob, :]),
              r=[("ot", ob, q) for q in range(4)], w=[("outdone", i)], dma=True, sem=("ost", ob))
    P.add("sp", None, r=[("outdone", i) for i in range(8)])
    P.emit()
    st.close()
    return nc, P.stats


def _host_consts(inputs, core):
    f32 = np.float32
    cst = np.zeros((128, NCST), f32)
    cst[:, C_IDF:C_IDF + 128] = np.eye(128, dtype=f32)
    cst[:, C_ONES:C_ONES + 128] = 1.0
    cst[:, C_TRI:C_TRI + 128] = np.triu(np.ones((128, 128), f32))
    par = np.zeros((128, NPAR), f32)
    norms = [inputs["ffn1_norm"][0], inputs["mix_norm"][0], inputs["ffn2_norm"][0], inputs["ple_norm"][0], inputs["final_norm"]]
    for i, w in enumerate(norms):
        par[:, P_NW + 16 * i:P_NW + 16 * (i + 1)] = np.asarray(w, f32).reshape(16, 128).T
    par[:, P_SCW:P_SCW + 48] = np.asarray(inputs["sc_conv_w"][0], f32).reshape(3, 16, 128).transpose(2, 1, 0).reshape(128, 48)
    par[:, P_SSDW:P_SSDW + 96] = np.asarray(inputs["ssd_conv_w"][0], f32).reshape(4, 24, 128).transpose(2, 1, 0).reshape(128, 96)
    par[:, P_SSDB:P_SSDB + 24] = np.asarray(inputs["ssd_conv_b"][0], f32).reshape(24, 128).T
    par[:, P_DTB:P_DTB + 32] = np.asarray(inputs["ssd_dt_bias"][0], f32)[None, :]
    par[:, P_ALOG:P_ALOG + 32] = np.asarray(inputs["ssd_a_log"][0], f32)[None, :]
    par[:, P_DSK:P_DSK + 32] = np.asarray(inputs["ssd_d"][0], f32)[None, :]
    par[:, P_SSDN:P_SSDN + 16] = np.asarray(inputs["ssd_norm"][0], f32).reshape(16, 128).T
    seq, seg = core // 4, core % 4
    for r in range(8):
        par[:, P_SEG + r] = 1.0 if (r // 4 == seq and r % 4 < seg) else 0.0
    par[:, P_EPS] = EPS
    par[:, P_ONE] = 1.0
    cst[:, C_PAR:] = par
    cstb = np.zeros((128, NCSTB), f32)
    cstb[:, B_ID:B_ID + 128] = np.eye(128, dtype=f32)
    cstb[:, B_MEAN:B_MEAN + 128] = 1.0 / D
    j = np.arange(128)[:, None]
    l = np.arange(128)[None, :]
    nm = np.where(l < j, -1e30, 0.0).astype(f32)
    cstb[:, B_NEG:B_NEG + 512] = np.tile(nm, (1, 4))
    return cst, cstb


_CACHE = {}


def make_in_maps(inputs, ncores=NCORES):
    f32 = np.float32
    x = np.asarray(inputs["x"], f32)
    p = np.asarray(inputs["p"], f32)[0]
    wmi = np.ascontiguousarray(np.asarray(inputs["mix_w_in"], f32)[0])
    wdt = np.ascontiguousarray(wmi[:, 11264:11296].reshape(16, 128, 32).transpose(1, 0, 2).reshape(128, 512))
    shared = {
        "wdt": wdt,
        "ffn1_w_in": np.ascontiguousarray(np.asarray(inputs["ffn1_w_in"], f32)[0]),
        "ffn1_w_out": np.ascontiguousarray(np.asarray(inputs["ffn1_w_out"], f32)[0]),
        "mix_w_in": wmi,
        "mix_w_out": np.ascontiguousarray(np.asarray(inputs["mix_w_out"], f32)[0]),
        "ffn2_w_in": np.ascontiguousarray(np.asarray(inputs["ffn2_w_in"], f32)[0]),
        "ffn2_w_out": np.ascontiguousarray(np.asarray(inputs["ffn2_w_out"], f32)[0]),
        "ple_w_gate": np.ascontiguousarray(np.asarray(inputs["ple_w_gate"], f32)[0]),
        "ple_w_proj": np.ascontiguousarray(np.asarray(inputs["ple_w_proj"], f32)[0]),
    }
    maps = []
    for c in range(ncores):
        seq, seg = c // 4, c % 4
        t0 = seg * T
        xc = np.zeros((TH, D), f32)
        xc[:T] = x[seq, t0:t0 + T]
        if seg > 0:
            xc[T:] = x[seq, t0 - HALO:t0]
        cst, cstb = _host_consts(inputs, c)
        m = {"xc": xc, "pc": np.ascontiguousarray(p[seq, t0:t0 + T]), "cst": cst, "cstb": cstb}
        m.update(shared)
        maps.append(m)
    return maps


def kernel(**inputs):
    if "nc" not in _CACHE:
        _CACHE["nc"] = build_program()[0]
    nc = _CACHE["nc"]
    maps = make_in_maps(inputs)
    res = run_bass_kernel_spmd(nc, maps, core_ids=list(range(NCORES)))
    out = np.zeros((2, 4096, D), np.float32)
    for c in range(NCORES):
        out[c // 4, (c % 4) * T:(c % 4 + 1) * T] = res.results[c]["out"]
    return out
```

```python
import numpy as np
from contextlib import ExitStack
import concourse.bass as bass
import concourse.mybir as mybir
from concourse.bass_utils import run_bass_kernel_spmd

F32 = mybir.dt.float32
BF16 = mybir.dt.bfloat16
U8 = mybir.dt.uint8
AF = mybir.ActivationFunctionType
ALU = mybir.AluOpType

ENGS = ("pe", "act", "dve", "pool", "sp")


class Op:
    __slots__ = ("eng", "fn", "r", "w", "dma", "sem", "deps", "signal", "sigval", "waits", "idx", "bar", "inc")

    def __init__(self, eng, fn, r, w, dma, sem, bar=False, inc=16):
        self.inc = inc
        self.eng = eng
        self.fn = fn
        self.r = tuple(r)
        self.w = tuple(w)
        self.dma = dma
        self.sem = sem
        self.bar = bar
        self.deps = ()
        self.signal = False
        self.sigval = 0
        self.waits = ()


class Prog:
    def __init__(self, nc):
        self.nc = nc
        self.ops = []

    def add(self, eng, fn, r=(), w=(), dma=False, sem=None, inc=16):
        o = Op(eng, fn, r, w, dma, sem, inc=inc)
        self.ops.append(o)
        return o

    def barrier(self, engs=ENGS, dma=True):
        for e in engs:
            o = Op(e, None, (), (), False, None, bar=True)
            o.inc = 1 if dma else 0
            self.ops.append(o)

    def analyze(self):
        last_w = {}
        readers = {}
        last_eng = {}
        last_dma = {}
        for i, o in enumerate(self.ops):
            o.idx = i
            deps = {}
            if o.bar:
                for p in last_eng.values():
                    deps[p.idx] = p
                if o.inc:
                    for p in last_dma.values():
                        deps[p.idx] = p
            for k in o.r:
                p = last_w.get(k)
                if p is not None:
                    deps[p.idx] = p
            for k in o.w:
                p = last_w.get(k)
                if p is not None:
                    deps[p.idx] = p
                rd = readers.get(k)
                if rd:
                    for p in rd.values():
                        deps[p.idx] = p
            for k in o.r:
                rd = readers.setdefault(k, {})
                rd[("d", i) if o.dma else o.eng] = o
            for k in o.w:
                last_w[k] = o
                readers[k] = {}
            deps.pop(i, None)
            dl = []
            for p in deps.values():
                if (not p.dma) and p.eng == "pe" and o.eng == "pe" and not o.dma:
                    continue
                if p.fn is None:
                    continue
                dl.append(p)
                if not p.dma:
                    p.signal = True
            o.deps = dl
            if o.dma:
                last_dma[o.sem] = o
            elif o.fn is not None:
                last_eng[o.eng] = o
        cnt = {e: 0 for e in ENGS}
        dcnt = {}
        for o in self.ops:
            if o.dma:
                dcnt[o.sem] = dcnt.get(o.sem, 0) + o.inc
                o.sigval = dcnt[o.sem]
            elif o.signal:
                cnt[o.eng] += 1
                o.sigval = cnt[o.eng]
        self.dma_keys = list(dcnt.keys())
        waited = {e: {} for e in ENGS}
        nw = 0
        for o in self.ops:
            need = {}
            for p in o.deps:
                key = ("d", p.sem) if p.dma else ("e", p.eng)
                if p.sigval > need.get(key, 0):
                    need[key] = p.sigval
            ws = []
            wd = waited[o.eng]
            for key, v in need.items():
                if v > wd.get(key, 0):
                    wd[key] = v
                    ws.append((key, v))
            o.waits = ws
            nw += len(ws)
        self.stats = dict(n_ops=len(self.ops), n_waits=nw, sig=cnt, ndma_sems=len(dcnt))

    def emit(self):
        nc = self.nc
        self.analyze()
        byeng = {e: [] for e in ENGS}
        for o in self.ops:
            byeng[o.eng].append(o)
        with ExitStack() as st:
            esem = {e: st.enter_context(nc.semaphore("se_" + e)) for e in ENGS}
            dsem = {k: st.enter_context(nc.semaphore("sd_%d" % i)) for i, k in enumerate(self.dma_keys)}
            block = st.enter_context(nc.Block())

            def mk(ename):
                def f(eng):
                    for o in byeng[ename]:
                        for (key, v) in o.waits:
                            s = dsem[key[1]] if key[0] == "d" else esem[key[1]]
                            eng.wait_ge(s, v)
                        if o.fn is None:
                            continue
                        ins = o.fn(eng)
                        if o.dma:
                            ins.then_inc(dsem[o.sem], o.inc)
                        elif o.signal:
                            ins.then_inc(esem[ename], 1)
                return f

            block.tensor(mk("pe"))
            block.scalar(mk("act"))
            block.vector(mk("dve"))
            block.gpsimd(mk("pool"))
            block.sync(mk("sp"))


D = 2048
DT = 16
DFF = 5632
FT = 44
T = 1024
HALO = 8
TH = T + HALO
T3 = [(0, 344), (344, 344), (688, 344)]
T2 = [(0, 512), (512, 512)]
EPS = 1e-6
NCORES = 8

OFF_H = 0
OFF_U = 66048
OFF_HID = 99072
OFF_RING = 132096
NSLOT = 6
SLOT = 8192
OFF_R2 = OFF_RING + 3 * SLOT
OFF_C = OFF_RING + NSLOT * SLOT
OFF_T = OFF_C + 6144
SB_BYTES = 212800

C_IDF = 0
C_ONES = 128
C_TRI = 256
C_PAR = 384
P_NW = 0
P_SCW = 80
P_SSDW = 128
P_SSDB = 224
P_DTB = 248
P_ALOG = 280
P_DSK = 312
P_SSDN = 344
P_SEG = 360
P_EPS = 368
P_ONE = 369
NPAR = 384
NCST = C_PAR + NPAR
B_ID = 0
B_MEAN = 128
B_NEG = 256
NCSTB = 768


def build_program(stage=5, ncores=NCORES, skip_mixer=False):
    nc = bass.Bass("TRN2", target_bir_lowering=False)
    dram = {}

    def din(name, shape):
        dram[name] = nc.dram_tensor(name, shape, F32, kind="ExternalInput").ap()
        return dram[name]

    x_d = din("xc", [TH, D])
    p_d = din("pc", [T, 256])
    cst_d = din("cst", [128, NCST])
    cstb_d = din("cstb", [128, NCSTB])
    wdt_d = din("wdt", [128, 16 * 32])
    w1i = din("ffn1_w_in", [D, 2 * DFF])
    w1o = din("ffn1_w_out", [DFF, D])
    wmi = din("mix_w_in", [D, 11296])
    wmo = din("mix_w_out", [4096, D])
    w2i = din("ffn2_w_in", [D, 2 * DFF])
    w2o = din("ffn2_w_out", [DFF, D])
    wpg = din("ple_w_gate", [D, D])
    wpp = din("ple_w_proj", [256, D])
    out_d = nc.dram_tensor("out", [T, D], F32, kind="ExternalOutput").ap()
    h_sp = nc.dram_tensor("h_spill", [128, DT * T], F32).ap()
    ya_sp = nc.dram_tensor("ya_spill", [128, DT * T], BF16).ap()
    cc_in = nc.dram_tensor("cc_in", [128, 2080], F32)
    cc_out = nc.dram_tensor("cc_out", [ncores * 128, 2080], F32)

    P = Prog(nc)
    st = ExitStack()
    raw = st.enter_context(nc.sbuf_tensor("raw", [128, SB_BYTES], U8))
    ps = st.enter_context(nc.psum_tensor("ps", [128, 8, 512], F32))

    def V(off, shape, dt):
        n = int(np.prod(shape[1:])) * (4 if dt == F32 else 2)
        v = raw[:, off:off + n].bitcast(dt)
        if len(shape) == 3:
            v = v.rearrange("p (a b) -> p a b", a=shape[1])
        elif len(shape) == 4:
            v = v.rearrange("p (a b c) -> p a b c", a=shape[1], b=shape[2])
        return v

    def psb(b):
        return ps[:, b, :].bitcast(BF16)

    h_fm = V(OFF_H, [128, DT, TH], F32)
    u_fm = V(OFF_U, [128, DT, TH], BF16)
    hid = V(OFF_HID, [128, 16, TH], BF16)
    cst = V(OFF_C, [128, NCST], F32)
    cstb = V(OFF_C + 3072, [128, NCSTB], BF16)
    wdt = V(OFF_C + 4608, [128, 16, 32], BF16)
    idf = cst[:, C_IDF:C_IDF + 128]
    ones_f = cst[:, C_ONES:C_ONES + 128]
    tri_f = cst[:, C_TRI:C_TRI + 128]
    par = cst[:, C_PAR:C_PAR + NPAR]
    idb = cstb[:, B_ID:B_ID + 128]
    mean_b = cstb[:, B_MEAN:B_MEAN + 128]
    negm = cstb[:, B_NEG:B_NEG + 512]
    eps_c = par[:, P_EPS:P_EPS + 1]
    one_c = par[:, P_ONE:P_ONE + 1]
    rstd_b = V(OFF_T, [128, TH], F32)
    sqb = V(OFF_T + 4128, [128, 2, TH], BF16)
    stmp = V(OFF_T + 8256, [128, 3, 512], F32)
    gtmp = V(OFF_T + 14400, [128, 2, 512], F32)

    P.add("sp", lambda e: e.dma_start(out=cst, in_=cst_d), w=["cst"], dma=True, sem="cst")
    P.add("pool", lambda e: e.dma_start(out=cstb, in_=cstb_d), w=["cstb"], dma=True, sem="cstb")
    P.add("pool", lambda e: e.dma_start(out=wdt, in_=wdt_d.rearrange("p (a b) -> p a b", a=16)), w=["wdt"], dma=True, sem="wdt")

    ring = {"n": 0, "ns": NSLOT}

    def wload(src_ap, shape):
        s = ring["n"] % ring["ns"]
        ring["n"] += 1
        v = V(OFF_RING + s * SLOT, shape, BF16)
        P.add("pool", lambda e: e.dma_start(out=v, in_=src_ap), w=[("ring", s)], dma=True, sem=("ring", s))
        return v, ("ring", s)

    def win_cols(w, c0, ncol=256):
        return w.rearrange("(kt p) f -> p kt f", p=128)[:, :, c0:c0 + ncol]

    xst = V(OFF_HID, [128, 4, D], F32)
    tcount = [0]
    for i in range(9):
        rows = 128 if i < 8 else HALO
        sb = i % 4
        P.add("sp", lambda e, i=i, rows=rows, sb=sb: e.dma_start(out=xst[0:rows, sb, :], in_=x_d[i * 128:i * 128 + rows, :]),
              w=[("xst", sb)], dma=True, sem=("xst", sb))
        for q in range(4):
            bk = tcount[0] % 6
            tcount[0] += 1
            for m in range(4):
                f = 4 * q + m
                P.add("pe", lambda e, bk=bk, m=m, f=f, rows=rows, sb=sb: e.transpose(
                    ps[:, bk, m * 128:m * 128 + rows], xst[0:rows, sb, f * 128:(f + 1) * 128], idf[0:rows, 0:rows]),
                    r=[("xst", sb), "cst"], w=[("ps", bk)])
            eng = "act" if (tcount[0] % 2) else "dve"
            src = lambda bk=bk, rows=rows: ps[:, bk, :].rearrange("p (m c) -> p m c", m=4)[:, :, 0:rows]
            dst = lambda q=q, i=i, rows=rows: h_fm[:, 4 * q:4 * q + 4, i * 128:i * 128 + rows]
            if eng == "act":
                P.add("act", lambda e, src=src, dst=dst: e.activation(out=dst(), in_=src(), func=AF.Copy),
                      r=[("ps", bk)], w=[("h", 4 * q + m) for m in range(4)])
            else:
                P.add("dve", lambda e, src=src, dst=dst: e.tensor_copy(dst(), src()),
                      r=[("ps", bk)], w=[("h", 4 * q + m) for m in range(4)])

    def rmsnorm(wi, tiles, out_fn=None, w_keys=("u",), after_tile=None):
        for ti, (c0, n) in enumerate(tiles):
            bk = 6 + ti % 2
            for f in range(DT):
                sbuf = f % 2
                P.add("act", lambda e, f=f, c0=c0, n=n, sbuf=sbuf: e.activation(
                    out=sqb[:, sbuf, 0:n], in_=h_fm[:, f, c0:c0 + n], func=AF.Square),
                    r=[("h", f)], w=[("sq", sbuf)])
                P.add("pe", lambda e, f=f, n=n, sbuf=sbuf, bk=bk: e.matmul(
                    ps[:, bk, 0:n], lhsT=mean_b, rhs=sqb[:, sbuf, 0:n], start=(f == 0), stop=(f == DT - 1)),
                    r=[("sq", sbuf), "cstb"], w=[("ps", bk)])
            P.add("act", lambda e, c0=c0, n=n, bk=bk: e.activation(
                out=rstd_b[:, c0:c0 + n], in_=ps[:, bk, 0:n], func=AF.Sqrt, bias=eps_c, scale=1.0),
                r=[("ps", bk), "cst"], w=[("rstd", ti)])
            P.add("dve", lambda e, c0=c0, n=n: e.reciprocal(rstd_b[:, c0:c0 + n], rstd_b[:, c0:c0 + n]),
                  r=[("rstd", ti)], w=[("rstd", ti)])
            for f in range(DT):
                o_ap = (lambda f=f, c0=c0, n=n: u_fm[:, f, c0:c0 + n]) if out_fn is None else (lambda f=f, c0=c0, n=n: out_fn(f, c0, n))
                wk = list(w_keys) if out_fn is None else [("h", f)]
                P.add("dve", lambda e, f=f, c0=c0, n=n, o_ap=o_ap: e.scalar_tensor_tensor(
                    out=o_ap(), in0=h_fm[:, f, c0:c0 + n], scalar=par[:, P_NW + wi * 16 + f:P_NW + wi * 16 + f + 1],
                    in1=rstd_b[:, c0:c0 + n], op0=ALU.mult, op1=ALU.mult),
                    r=[("h", f), ("rstd", ti), "cst"], w=wk)
            if after_tile is not None:
                after_tile(ti, c0, n)

    def ffn(w_in, w_out, tiles):
        blocks = [(b * 8, min(8, FT - b * 8)) for b in range((FT + 7) // 8)]
        unit = [0]
        oacc = [0]

        def emit_in(bi):
            j0, kb = blocks[bi]
            buf = bi % 2
            for pr in range(kb // 2):
                wg, rg = wload(win_cols(w_in, (j0 + 2 * pr) * 128), [128, 16, 256])
                wu, ru = wload(win_cols(w_in, DFF + (j0 + 2 * pr) * 128), [128, 16, 256])
                for jj2 in range(2):
                    jj = 2 * pr + jj2
                    for ti, (c0, n) in enumerate(tiles):
                        q = unit[0] % 3
                        unit[0] += 1
                        gb, ub = 2 * q, 2 * q + 1
                        for k in range(16):
                            P.add("pe", lambda e, k=k, wg=wg, jj2=jj2, c0=c0, n=n, gb=gb: e.matmul(
                                ps[:, gb, 0:n], lhsT=wg[:, k, jj2 * 128:(jj2 + 1) * 128], rhs=u_fm[:, k, c0:c0 + n],
                                start=(k == 0), stop=(k == 15)), r=[rg, "u"], w=[("ps", gb)])
                        for k in range(16):
                            P.add("pe", lambda e, k=k, wu=wu, jj2=jj2, c0=c0, n=n, ub=ub: e.matmul(
                                ps[:, ub, 0:n], lhsT=wu[:, k, jj2 * 128:(jj2 + 1) * 128], rhs=u_fm[:, k, c0:c0 + n],
                                start=(k == 0), stop=(k == 15)), r=[ru, "u"], w=[("ps", ub)])
                        P.add("act", lambda e, q=q, n=n, gb=gb: e.activation(out=stmp[:, q, 0:n], in_=ps[:, gb, 0:n], func=AF.Silu),
                              r=[("ps", gb)], w=[("stmp", q)])
                        P.add("dve", lambda e, q=q, n=n, ub=ub, buf=buf, jj=jj, c0=c0: e.tensor_tensor(
                            out=hid[:, buf * 8 + jj, c0:c0 + n], in0=stmp[:, q, 0:n], in1=ps[:, ub, 0:n], op=ALU.mult),
                            r=[("stmp", q), ("ps", ub)], w=[("hid", buf, jj)])

        def emit_out(bi):
            j0, kb = blocks[bi]
            buf = bi % 2
            wos = []
            for pr in range(kb // 2):
                src = w_out.rearrange("(kt p) f -> p kt f", p=128)[:, j0 + 2 * pr:j0 + 2 * pr + 2, :]
                wos.append(wload(src, [128, 2, D]))
            for f in range(DT):
                for ti, (c0, n) in enumerate(tiles):
                    ob = 6 + oacc[0] % 2
                    oacc[0] += 1
                    for kk in range(kb):
                        wv, rk = wos[kk // 2]
                        P.add("pe", lambda e, wv=wv, kk=kk, f=f, c0=c0, n=n, ob=ob, buf=buf, kb=kb: e.matmul(
                            ps[:, ob, 0:n], lhsT=wv[:, kk % 2, f * 128:(f + 1) * 128], rhs=hid[:, buf * 8 + kk, c0:c0 + n],
                            start=(kk == 0), stop=(kk == kb - 1)), r=[rk, ("hid", buf, kk)], w=[("ps", ob)])
                    P.add("dve", lambda e, f=f, c0=c0, n=n, ob=ob: e.scalar_tensor_tensor(
                        out=h_fm[:, f, c0:c0 + n], in0=ps[:, ob, 0:n], scalar=0.5, in1=h_fm[:, f, c0:c0 + n],
                        op0=ALU.mult, op1=ALU.add), r=[("ps", ob), ("h", f)], w=[("h", f)])

        nb = len(blocks)
        emit_in(0)
        for bi in range(nb):
            if bi + 1 < nb:
                emit_in(bi + 1)
            emit_out(bi)


    def bc_mid(ap2, n):
        return ap2.unsqueeze(1).to_broadcast([128, n, ap2.shape[1]])

    def bc_last(ap2, n):
        return ap2.unsqueeze(2).to_broadcast([128, ap2.shape[1], n])

    def mixer():
        X_tm = V(OFF_H, [128, 8, 2048], BF16)
        zs_tm = V(OFF_H + 32768, [128, 8, 2048], BF16)
        yb_fm = V(OFF_U, [128, DT, T], BF16)
        ya_fm = V(OFF_HID, [128, DT, T], BF16)
        B_fm = V(OFF_R2, [128, 4, T], BF16)
        C_fm = V(OFF_R2 + 8192, [128, 4, T], BF16)
        st_bf = V(OFF_R2 + 16384, [128, 2048], BF16)
        dtv = V(OFF_R2 + 20480, [128, 8, 32], F32)
        A_tm = V(OFF_R2 + 21504, [128, 8, 32], F32)
        acs = V(OFF_R2 + 22528, [128, 8, 32], F32)
        tot = V(OFF_R2 + 23552, [128, 8, 32], F32)
        sm = lambda i: V(OFF_T + 1024 * i, [128, 8, 32], F32)
        nacs, eacs, wd, cd, wseg, suf, tmpa, tmpb = [sm(i) for i in range(8)]
        a_b = V(OFF_T + 8192, [128, 32], F32)
        totseg = V(OFF_T + 8320, [128, 32], F32)
        ss4 = V(OFF_T + 8448, [128, 8], F32)
        rs4 = V(OFF_T + 8480, [128, 8], F32)
        cfac = V(OFF_T + 8512, [128, 32], F32)
        A = OFF_HID
        pcb = V(A, [128, TH], F32)
        cvb = V(A + 4128, [128, T], F32)
        xsb = V(A + 8224, [128, 2, T], BF16)
        csb = V(A + 8224, [128, TH], F32)
        yast = V(A + 12352, [128, 2, T], BF16)
        btmA = V(A + 16448, [128, 2, 512], BF16)
        xwA = V(A + 18496, [128, 2, 512], BF16)
        sA = V(A + 20544, [128, 2048], F32)
        dsk = par[:, P_DSK:P_DSK + 32]

        ring["ns"] = 3
        rmsnorm(1, T3)
        P.add("sp", lambda e: e.dma_start(out=h_sp.rearrange("p (f t) -> p f t", f=DT), in_=h_fm[:, :, 0:T]),
              r=[("h", f) for f in range(DT)], w=["hsp"], dma=True, sem="hsp")

        cnt = [0]
        xcnt = [0]

        def conv_tile(src_cols, wbase, ntap, j_w):
            off0 = 8 - (ntap - 1)
            P.add("dve", lambda e: e.tensor_scalar(out=cvb, in0=pcb[:, off0:off0 + T], scalar1=par[:, wbase + j_w * ntap:wbase + j_w * ntap + 1],
                                                   scalar2=None, op0=ALU.mult), r=["pcb", "cst"], w=["cvb"])
            for k in range(1, ntap):
                P.add("dve", lambda e, k=k: e.scalar_tensor_tensor(out=cvb, in0=pcb[:, off0 + k:off0 + k + T],
                                                                   scalar=par[:, wbase + j_w * ntap + k:wbase + j_w * ntap + k + 1], in1=cvb, op0=ALU.mult, op1=ALU.add),
                      r=["pcb", "cvb", "cst"], w=["cvb"])

        def proj3(wv, rk, jj2):
            s3 = cnt[0] % 2
            cnt[0] += 1
            for ti, (c0, n) in enumerate(T3):
                for k in range(16):
                    P.add("pe", lambda e, k=k, ti=ti, c0=c0, n=n, s3=s3: e.matmul(
                        ps[:, 3 * s3 + ti, 0:n], lhsT=wv[:, k, jj2 * 128:(jj2 + 1) * 128], rhs=u_fm[:, k, c0:c0 + n],
                        start=(k == 0), stop=(k == 15)), r=[rk, "u"], w=[("ps", 3 * s3 + ti)])
            return s3

        def evac3(s3, dst, key):
            rr = [("ps", 3 * s3 + t) for t in range(3)]
            P.add("act", lambda e: e.activation(out=dst[:, 8:696].rearrange("p (a b) -> p a b", a=2), in_=ps[:, 3 * s3:3 * s3 + 2, 0:344], func=AF.Copy), r=rr, w=[key])
            P.add("act", lambda e: e.activation(out=dst[:, 696:1032], in_=ps[:, 3 * s3 + 2, 0:336], func=AF.Copy), r=rr, w=[key])
            P.add("act", lambda e: e.activation(out=dst[:, 0:8], in_=ps[:, 3 * s3 + 2, 336:344], func=AF.Copy), r=rr, w=[key])

        jorder = list(range(16, 24)) + list(range(0, 16))
        pend = [None]
        for pi in range(12):
            j0 = jorder[2 * pi]
            if j0 == 0:
                P.barrier(("act", "dve"))
            wv, rk = wload(win_cols(wmi, 8192 + j0 * 128), [128, 16, 256])
            for jj2 in range(2):
                j = j0 + jj2
                s3 = proj3(wv, rk, jj2)
                if pend[0] is not None:
                    pend[0]()
                    pend[0] = None
                evac3(s3, pcb, "pcb")
                conv_tile(None, P_SSDW, 4, j)
                bias = par[:, P_SSDB + j:P_SSDB + j + 1]
                if j >= 20:
                    P.add("act", lambda e, j=j, bias=bias: e.activation(out=C_fm[:, j - 20, :], in_=cvb, func=AF.Silu, bias=bias), r=["cvb", "cst"], w=[("C", j - 20)])
                elif j >= 16:
                    P.add("act", lambda e, j=j, bias=bias: e.activation(out=B_fm[:, j - 16, :], in_=cvb, func=AF.Silu, bias=bias), r=["cvb", "cst"], w=[("B", j - 16)])
                else:
                    xb = xcnt[0] % 2
                    xcnt[0] += 1
                    P.add("act", lambda e, xb=xb, bias=bias: e.activation(out=xsb[:, xb, :], in_=cvb, func=AF.Silu, bias=bias), r=["cvb", "cst"], w=[("xsb", xb)])
                    bk = 6 + xb

                    def emit_tr(j=j, xb=xb, bk=bk):
                        for i in range(8):
                            P.add("pe", lambda e, i=i: e.transpose(psb(bk)[:, i * 128:(i + 1) * 128], xsb[:, xb, i * 128:(i + 1) * 128], idb),
                                  r=[("xsb", xb), "cstb"], w=[("ps", bk)])
                        P.add("dve", lambda e: e.tensor_copy(X_tm[:, :, j * 128:(j + 1) * 128], psb(bk)[:, 0:1024].rearrange("p (i c) -> p i c", i=8)),
                              r=[("ps", bk)], w=[("X", j // 4)])
                    pend[0] = emit_tr
        if pend[0] is not None:
            pend[0]()
            pend[0] = None

        for i in range(8):
            for k in range(16):
                P.add("pe", lambda e, i=i, k=k: e.matmul(ps[:, 7, i * 32:(i + 1) * 32], lhsT=u_fm[:, k, i * 128:(i + 1) * 128], rhs=wdt[:, k, :],
                                                         start=(k == 0), stop=(k == 15)), r=["u", "wdt"], w=[("ps", 7)])
        f256 = lambda v: v.rearrange("p a b -> p (a b)")
        P.add("dve", lambda e: e.tensor_tensor(out=dtv, in0=ps[:, 7, 0:256].rearrange("p (a b) -> p a b", a=8), in1=bc_mid(par[:, P_DTB:P_DTB + 32], 8), op=ALU.add),
              r=[("ps", 7), "cst"], w=["dtv"])
        P.add("act", lambda e: e.activation(out=f256(tmpa), in_=f256(dtv), func=AF.Abs), r=["dtv"], w=["tmpa"])
        P.add("act", lambda e: e.activation(out=f256(tmpa), in_=f256(tmpa), func=AF.Exp, scale=-1.0), r=["tmpa"], w=["tmpa"])
        P.add("act", lambda e: e.activation(out=f256(tmpa), in_=f256(tmpa), func=AF.Ln, bias=one_c, scale=1.0), r=["tmpa", "cst"], w=["tmpa"])
        P.add("dve", lambda e: e.tensor_scalar_max(out=f256(dtv), in0=f256(dtv), scalar1=0.0), r=["dtv"], w=["dtv"])
        P.add("dve", lambda e: e.tensor_tensor(out=f256(dtv), in0=f256(dtv), in1=f256(tmpa), op=ALU.add), r=["dtv", "tmpa"], w=["dtv"])
        P.add("act", lambda e: e.activation(out=a_b, in_=par[:, P_ALOG:P_ALOG + 32], func=AF.Exp), r=["cst"], w=["a_b"])
        P.add("dve", lambda e: e.tensor_scalar(out=a_b, in0=a_b, scalar1=-1.0, scalar2=None, op0=ALU.mult), r=["a_b"], w=["a_b"])
        P.add("dve", lambda e: e.tensor_tensor(out=A_tm, in0=dtv, in1=bc_mid(a_b, 8), op=ALU.mult), r=["dtv", "a_b"], w=["A_tm"])
        for c in range(8):
            P.add("pe", lambda e, c=c: e.matmul(ps[:, 6, c * 32:(c + 1) * 32], lhsT=tri_f, rhs=A_tm[:, c, :], start=True, stop=True), r=["A_tm", "cst"], w=[("ps", 6)])
        for c in range(8):
            P.add("pe", lambda e, c=c: e.matmul(ps[:, 6, 256 + c * 32:256 + (c + 1) * 32], lhsT=ones_f, rhs=A_tm[:, c, :], start=True, stop=True), r=["A_tm", "cst"], w=[("ps", 6)])
        P.add("act", lambda e: e.activation(out=f256(acs), in_=ps[:, 6, 0:256], func=AF.Copy), r=[("ps", 6)], w=["acs"])
        P.add("act", lambda e: e.activation(out=f256(tot), in_=ps[:, 6, 256:512], func=AF.Copy), r=[("ps", 6)], w=["tot"])
        P.add("dve", lambda e: e.tensor_scalar(out=f256(nacs), in0=f256(acs), scalar1=-1.0, scalar2=None, op0=ALU.mult), r=["acs"], w=["nacs"])
        P.add("act", lambda e: e.activation(out=f256(eacs), in_=f256(acs), func=AF.Exp), r=["acs"], w=["eacs"])
        P.add("act", lambda e: e.activation(out=f256(cd), in_=f256(tot), func=AF.Exp), r=["tot"], w=["cd"])
        P.add("dve", lambda e: e.tensor_tensor(out=f256(tmpb), in0=f256(tot), in1=f256(acs), op=ALU.subtract), r=["tot", "acs"], w=["tmpb"])
        P.add("act", lambda e: e.activation(out=f256(wd), in_=f256(tmpb), func=AF.Exp), r=["tmpb"], w=["wd"])
        P.add("dve", lambda e: e.tensor_tensor(out=f256(wd), in0=f256(wd), in1=f256(dtv), op=ALU.mult), r=["wd", "dtv"], w=["wd"])
        P.add("dve", lambda e: e.memset(suf[:, 7, :], 0.0), w=["suf"])
        for c in range(6, -1, -1):
            P.add("dve", lambda e, c=c: e.tensor_tensor(out=suf[:, c, :], in0=suf[:, c + 1, :], in1=tot[:, c + 1, :], op=ALU.add), r=["suf", "tot"], w=["suf"])
        P.add("dve", lambda e: e.tensor_tensor(out=totseg, in0=suf[:, 0, :], in1=tot[:, 0, :], op=ALU.add), r=["suf", "tot"], w=["totseg"])
        P.add("dve", lambda e: e.tensor_tensor(out=f256(tmpb), in0=f256(tmpb), in1=f256(suf), op=ALU.add), r=["tmpb", "suf", "wd"], w=["tmpb"])
        P.add("act", lambda e: e.activation(out=f256(wseg), in_=f256(tmpb), func=AF.Exp), r=["tmpb"], w=["wseg"])
        P.add("dve", lambda e: e.tensor_tensor(out=f256(wseg), in0=f256(wseg), in1=f256(dtv), op=ALU.mult), r=["wseg", "dtv"], w=["wseg"])

        def btm_chunk(c, dstv, key):
            for g in range(4):
                P.add("pe", lambda e, g=g, c=c: e.transpose(psb(7)[:, g * 128:(g + 1) * 128], B_fm[:, g, c * 128:(c + 1) * 128], idb),
                      r=[("B", g), "cstb"], w=[("ps", 7)])
            P.add("act", lambda e: e.activation(out=dstv, in_=psb(7)[:, 0:512], func=AF.Copy), r=[("ps", 7)], w=[key])

        for c in range(8):
            bb = c % 2
            btm_chunk(c, btmA[:, bb, :], ("btmA", bb))
            for g in range(4):
                xb = (c * 4 + g) % 2
                P.add("dve", lambda e, c=c, g=g, xb=xb: e.tensor_tensor(
                    out=xwA[:, xb, :].rearrange("p (h d) -> p h d", h=8), in0=X_tm[:, c, g * 512:(g + 1) * 512].rearrange("p (h d) -> p h d", h=8),
                    in1=bc_last(wseg[:, c, 8 * g:8 * g + 8], 64), op=ALU.mult), r=[("X", g), "wseg"], w=[("xwA", xb)])
                P.add("pe", lambda e, c=c, g=g, xb=xb, bb=bb: e.matmul(ps[:, g, :], lhsT=btmA[:, bb, g * 128:(g + 1) * 128], rhs=xwA[:, xb, :],
                                                                       start=(c == 0), stop=(c == 7)), r=[("btmA", bb), ("xwA", xb)], w=[("ps", g)])
        for g in range(4):
            P.add("act" if g % 2 else "dve",
                  (lambda e, g=g: e.activation(out=sA[:, g * 512:(g + 1) * 512], in_=ps[:, g, :], func=AF.Copy)) if g % 2 else
                  (lambda e, g=g: e.tensor_copy(sA[:, g * 512:(g + 1) * 512], ps[:, g, :])), r=[("ps", g)], w=[("sA", g)])
        cin = cc_in.ap()
        P.add("sp", lambda e: e.dma_start(out=cin[:, 0:2048], in_=sA), r=[("sA", g) for g in range(4)], w=["cin0"], dma=True, sem="cin0")
        P.add("sp", lambda e: e.dma_start(out=cin[:, 2048:2080], in_=totseg), r=["totseg"], w=["cin1"], dma=True, sem="cin1")
        P.add("pool", lambda e: e.collective_compute("AllGather", ALU.bypass, replica_groups=[list(range(ncores))],
                                                     ins=[cc_in.ap().opt()], outs=[cc_out.ap().opt()]),
              r=["cin0", "cin1"], w=["cout"], dma=True, sem="cc", inc=1)
        P.barrier(("pe", "act", "dve"), dma=False)

        zc = [0]
        for cb in range(8):
            wv, rk = wload(win_cols(wmi, 6144 + cb * 256), [128, 16, 256])
            for i in range(8):
                bk = zc[0] % 8
                zc[0] += 1
                for k in range(16):
                    P.add("pe", lambda e, k=k, i=i, bk=bk, wv=wv: e.matmul(ps[:, bk, 0:256], lhsT=u_fm[:, k, i * 128:(i + 1) * 128], rhs=wv[:, k, :],
                                                                         start=(k == 0), stop=(k == 15)), r=[rk, "u"], w=[("ps", bk)])
                P.add("act", lambda e, i=i, cb=cb, bk=bk: e.activation(out=zs_tm[:, i, cb * 256:(cb + 1) * 256], in_=ps[:, bk, 0:256], func=AF.Silu),
                      r=[("ps", bk)], w=[("zs", cb // 2)])

        yac = [0]
        for pi in range(8):
            j0 = 2 * pi
            wc, rc = wload(win_cols(wmi, 2048 + j0 * 128), [128, 16, 256])
            wx, rx = wload(win_cols(wmi, 4096 + j0 * 128), [128, 16, 256])
            wb, rb = wload(win_cols(wmi, j0 * 128), [128, 16, 256])
            for jj2 in range(2):
                j = j0 + jj2
                s3 = proj3(wc, rc, jj2)
                evac3(s3, csb, "csb")
                s3 = proj3(wx, rx, jj2)
                rr = [("ps", 3 * s3 + t) for t in range(3)]
                P.add("dve", lambda e, s3=s3: e.tensor_tensor(out=pcb[:, 8:696].rearrange("p (a b) -> p a b", a=2), in0=csb[:, 8:696].rearrange("p (a b) -> p a b", a=2),
                                                              in1=ps[:, 3 * s3:3 * s3 + 2, 0:344], op=ALU.mult), r=rr + ["csb"], w=["pcb"])
                P.add("dve", lambda e, s3=s3: e.tensor_tensor(out=pcb[:, 696:1032], in0=csb[:, 696:1032], in1=ps[:, 3 * s3 + 2, 0:336], op=ALU.mult), r=rr + ["csb"], w=["pcb"])
                P.add("dve", lambda e, s3=s3: e.tensor_tensor(out=pcb[:, 0:8], in0=csb[:, 0:8], in1=ps[:, 3 * s3 + 2, 336:344], op=ALU.mult), r=rr + ["csb"], w=["pcb"])
                conv_tile(None, P_SCW, 3, j)
                s3 = proj3(wb, rb, jj2)
                rr = [("ps", 3 * s3 + t) for t in range(3)]
                yb = yac[0] % 2
                yac[0] += 1
                P.add("dve", lambda e, s3=s3, yb=yb: e.tensor_tensor(out=yast[:, yb, 0:688].rearrange("p (a b) -> p a b", a=2), in0=cvb[:, 0:688].rearrange("p (a b) -> p a b", a=2),
                                                                     in1=ps[:, 3 * s3:3 * s3 + 2, 0:344], op=ALU.mult), r=rr + ["cvb"], w=[("yast", yb)])
                P.add("dve", lambda e, s3=s3, yb=yb: e.tensor_tensor(out=yast[:, yb, 688:1024], in0=cvb[:, 688:1024], in1=ps[:, 3 * s3 + 2, 0:336], op=ALU.mult),
                      r=rr + ["cvb"], w=[("yast", yb)])
                P.add("sp", lambda e, j=j, yb=yb: e.dma_start(out=ya_sp[:, j * T:(j + 1) * T], in_=yast[:, yb, :]), r=[("yast", yb)], w=[("yasp", j)],
                      dma=True, sem=("yast", yb))
        P.barrier(("pe", "act", "dve"))

        state = V(A, [128, 2048], F32)
        xd = V(A + 8192, [128, 2, 512], BF16)
        xdt = V(A + 10240, [128, 2, 512], BF16)
        btmB = V(A + 12288, [128, 2, 512], BF16)
        am = V(A + 14336, [128, 2, 512], F32)
        lt = V(A + 18432, [128, 2, 512], F32)
        mt = V(A + 22528, [128, 2, 512], BF16)
        cbt = V(A + 24576, [128, 512], F32)
        t1 = V(A + 26624, [128, 2, 512], F32)
        ybg = V(A + 30720, [128, 2, 512], BF16)
        lr = V(OFF_U, [128, 3, 2080], F32)
        cout = cc_out.ap()
        P.add("dve", lambda e: e.memset(state, 0.0), w=["state"])
        for n_, r_ in enumerate((0, 1, 2, 4, 5, 6)):
            lb = n_ % 3
            P.add("sp", lambda e, r_=r_, lb=lb: e.dma_start(out=lr[:, lb, :], in_=cout[r_ * 128:(r_ + 1) * 128, :]), r=["cout"], w=[("lr", lb)], dma=True, sem=("lr", lb))
            m_r = par[:, P_SEG + r_:P_SEG + r_ + 1]
            P.add("act", lambda e, lb=lb: e.activation(out=cfac, in_=lr[:, lb, 2048:2080], func=AF.Exp), r=[("lr", lb)], w=["cfac"])
            P.add("dve", lambda e: e.tensor_scalar(out=cfac, in0=cfac, scalar1=-1.0, scalar2=None, op0=ALU.add), r=["cfac"], w=["cfac"])
            P.add("dve", lambda e, m_r=m_r: e.tensor_scalar(out=cfac, in0=cfac, scalar1=m_r, scalar2=None, op0=ALU.mult), r=["cfac", "cst"], w=["cfac"])
            P.add("dve", lambda e: e.tensor_scalar(out=cfac, in0=cfac, scalar1=1.0, scalar2=None, op0=ALU.add), r=["cfac"], w=["cfac"])
            P.add("dve", lambda e: e.tensor_tensor(out=state.rearrange("p (h d) -> p h d", h=32), in0=state.rearrange("p (h d) -> p h d", h=32),
                                                   in1=bc_last(cfac, 64), op=ALU.mult), r=["state", "cfac"], w=["state"])
            P.add("dve", lambda e, lb=lb, m_r=m_r: e.scalar_tensor_tensor(out=state, in0=lr[:, lb, 0:2048], scalar=m_r, in1=state, op0=ALU.mult, op1=ALU.add),
                  r=[("lr", lb), "state", "cst"], w=["state"])
        for g in range(4):
            P.add("act", lambda e, g=g: e.activation(out=st_bf[:, g * 512:(g + 1) * 512], in_=state[:, g * 512:(g + 1) * 512], func=AF.Copy), r=["state"], w=[("sbf", g)])
        P.barrier(("pe", "act", "dve"))

        def emit_group(c, g, ch):
            cs = slice(c * 128, (c + 1) * 128)
            bb = c % 2
            acb, ydb, yob = 1 + ch, 3 + ch, 5 + ch
            Xg = lambda: X_tm[:, c, g * 512:(g + 1) * 512].rearrange("p (h d) -> p h d", h=8)
            P.add("dve", lambda e: e.tensor_tensor(out=xdt[:, ch, :].rearrange("p (h d) -> p h d", h=8), in0=Xg(),
                                                    in1=bc_last(dtv[:, c, 8 * g:8 * g + 8], 64), op=ALU.mult), r=[("X", g), "dtv"], w=[("xdt", ch)])
            P.add("dve", lambda e: e.tensor_tensor(out=xd[:, ch, :].rearrange("p (h d) -> p h d", h=8), in0=Xg(),
                                                    in1=bc_last(wd[:, c, 8 * g:8 * g + 8], 64), op=ALU.mult), r=[("X", g), "wd"], w=[("xd", ch)])
            for q2 in range(2):
                q = 2 * g + q2
                P.add("dve", lambda e, q=q: e.tensor_tensor(out=am[:, ch, :].rearrange("p (h l) -> p h l", h=4), in0=bc_mid(tri_f, 4),
                                                             in1=bc_last(A_tm[:, c, 4 * q:4 * q + 4], 128), op=ALU.mult), r=["A_tm", "cst"], w=[("am", ch)])
                P.add("pe", lambda e: e.matmul(ps[:, acb, :], lhsT=ones_f, rhs=am[:, ch, :], start=True, stop=False), r=[("am", ch), "cst"], w=[("ps", acb)])
                P.add("pe", lambda e: e.matmul(ps[:, acb, :], lhsT=idb, rhs=negm, start=False, stop=True), r=["cstb"], w=[("ps", acb)])
                for hh in range(4):
                    P.add("act", lambda e, q=q, hh=hh: e.activation(out=lt[:, ch, hh * 128:(hh + 1) * 128], in_=ps[:, acb, hh * 128:(hh + 1) * 128],
                                                                    func=AF.Exp, bias=nacs[:, c, 4 * q + hh:4 * q + hh + 1], scale=1.0),
                          r=[("ps", acb), "nacs"], w=[("lt", ch)])
                P.add("dve", lambda e: e.tensor_tensor(out=mt[:, ch, :].rearrange("p (h l) -> p h l", h=4), in0=lt[:, ch, :].rearrange("p (h l) -> p h l", h=4),
                                                       in1=bc_mid(cbt[:, g * 128:(g + 1) * 128], 4), op=ALU.mult), r=[("lt", ch), "cbt"], w=[("mt", ch)])
                for hh in range(4):
                    hl = 4 * q2 + hh
                    P.add("pe", lambda e, hh=hh, hl=hl: e.matmul(ps[:, ydb, hl * 64:(hl + 1) * 64], lhsT=mt[:, ch, hh * 128:(hh + 1) * 128],
                                                                 rhs=xdt[:, ch, hl * 64:(hl + 1) * 64], start=True, stop=True),
                          r=[("mt", ch), ("xdt", ch)], w=[("ps", ydb)])
            P.add("pe", lambda e: e.matmul(ps[:, yob, :], lhsT=C_fm[:, g, cs], rhs=st_bf[:, g * 512:(g + 1) * 512], start=True, stop=True),
                  r=[("C", g), ("sbf", g)], w=[("ps", yob)])
            t3v = lambda: t1[:, ch, :].rearrange("p (h d) -> p h d", h=8)
            P.add("dve", lambda e: e.tensor_tensor(out=t3v(), in0=ps[:, yob, :].rearrange("p (h d) -> p h d", h=8),
                                                   in1=bc_last(eacs[:, c, 8 * g:8 * g + 8], 64), op=ALU.mult), r=[("ps", yob), "eacs"], w=[("t1", ch)])
            P.add("dve", lambda e: e.tensor_tensor(out=t1[:, ch, :], in0=t1[:, ch, :], in1=ps[:, ydb, :], op=ALU.add), r=[("t1", ch), ("ps", ydb)], w=[("t1", ch)])
            P.add("dve", lambda e: e.tensor_tensor(out=lt[:, ch, :].rearrange("p (h d) -> p h d", h=8), in0=Xg(), in1=bc_last(dsk[:, 8 * g:8 * g + 8], 64), op=ALU.mult),
                  r=[("X", g), "cst", ("lt", ch)], w=[("lt", ch)])
            P.add("dve", lambda e: e.tensor_tensor(out=t1[:, ch, :], in0=t1[:, ch, :], in1=lt[:, ch, :], op=ALU.add), r=[("t1", ch), ("lt", ch)], w=[("t1", ch)])
            P.add("dve", lambda e: e.tensor_tensor(out=t1[:, ch, :], in0=t1[:, ch, :], in1=zs_tm[:, c, g * 512:(g + 1) * 512], op=ALU.mult),
                  r=[("t1", ch), ("zs", g)], w=[("t1", ch)])
            P.add("dve", lambda e: e.memset(ss4[:, g:g + 1], 0.0), r=[("ss", g)], w=[("ss", g)])
            P.add("act", lambda e: e.activation(out=lt[:, ch, :], in_=t1[:, ch, :], func=AF.Square, accum_out=ss4[:, g:g + 1]),
                  r=[("t1", ch), ("lt", ch), ("ss", g)], w=[("ss", g), ("lt", ch)])
            P.add("act", lambda e: e.activation(out=rs4[:, g:g + 1], in_=ss4[:, g:g + 1], func=AF.Sqrt, bias=eps_c, scale=1.0 / 512), r=[("ss", g), "cst"], w=[("rs", g)])
            P.add("dve", lambda e: e.reciprocal(rs4[:, g:g + 1], rs4[:, g:g + 1]), r=[("rs", g)], w=[("rs", g)])
            P.add("act", lambda e: e.activation(out=ybg[:, ch, :], in_=t1[:, ch, :], func=AF.Copy, scale=rs4[:, g:g + 1]),
                  r=[("t1", ch), ("rs", g)], w=[("ybg", ch)])
            for m in range(4):
                P.add("pe", lambda e, m=m: e.transpose(psb(acb)[:, m * 128:(m + 1) * 128], ybg[:, ch, m * 128:(m + 1) * 128], idb), r=[("ybg", ch), "cstb"], w=[("ps", acb)])
            P.add("dve", lambda e: e.tensor_tensor(out=yb_fm[:, 4 * g:4 * g + 4, cs], in0=psb(acb)[:, 0:512].rearrange("p (m t) -> p m t", m=4),
                                                   in1=bc_last(par[:, P_SSDN + 4 * g:P_SSDN + 4 * g + 4], 128), op=ALU.mult), r=[("ps", acb), "cst"], w=[("yb", g)])
            if c < 7:
                P.add("pe", lambda e: e.matmul(ps[:, yob, :], lhsT=btmB[:, bb, g * 128:(g + 1) * 128], rhs=xd[:, ch, :], start=True, stop=True),
                      r=[("btmB", bb), ("xd", ch)], w=[("ps", yob)])
                sg = lambda: state[:, g * 512:(g + 1) * 512]
                P.add("dve", lambda e: e.tensor_tensor(out=sg().rearrange("p (h d) -> p h d", h=8), in0=sg().rearrange("p (h d) -> p h d", h=8),
                                                        in1=bc_last(cd[:, c, 8 * g:8 * g + 8], 64), op=ALU.mult), r=["cd", ("st", g)], w=[("st", g)])
                P.add("dve", lambda e: e.tensor_tensor(out=sg(), in0=sg(), in1=ps[:, yob, :], op=ALU.add), r=[("st", g), ("ps", yob)], w=[("st", g)])
                P.add("act", lambda e: e.activation(out=st_bf[:, g * 512:(g + 1) * 512], in_=sg(), func=AF.Copy), r=[("st", g)], w=[("sbf", g)])

        for c in range(8):
            cs = slice(c * 128, (c + 1) * 128)
            bb = c % 2
            btm_chunk(c, btmB[:, bb, :], ("btmB", bb))
            for g in range(4):
                P.add("pe", lambda e, g=g, cs=cs: e.matmul(ps[:, 0, g * 128:(g + 1) * 128], lhsT=B_fm[:, g, cs], rhs=C_fm[:, g, cs], start=True, stop=True),
                      r=[("B", g), ("C", g)], w=[("ps", 0)])
            P.add("act", lambda e: e.activation(out=cbt, in_=ps[:, 0, :], func=AF.Copy), r=[("ps", 0)], w=["cbt"])
            chains = [[], []]
            main_ops = P.ops
            for g in range(4):
                P.ops = []
                emit_group(c, g, g % 2)
                chains[g % 2].extend(P.ops)
            P.ops = main_ops
            n0, n1 = len(chains[0]), len(chains[1])
            for i in range(max(n0, n1)):
                if i < n0:
                    P.ops.append(chains[0][i])
                if i < n1:
                    P.ops.append(chains[1][i])
        P.barrier(("pe", "act", "dve", "sp"))

        P.add("sp", lambda e: e.dma_start(out=h_fm[:, :, 0:T], in_=h_sp.rearrange("p (f t) -> p f t", f=DT)), r=["hsp"], w=[("h", f) for f in range(DT)], dma=True, sem="hrl")
        P.add("sp", lambda e: e.dma_start(out=ya_fm, in_=ya_sp.rearrange("p (f t) -> p f t", f=DT)), r=[("yasp", j) for j in range(16)], w=["ya"], dma=True, sem="yrl")
        oc = [0]
        for f in range(DT):
            wv, rk = wload(wmo.rearrange("(kt p) f -> p kt f", p=128)[:, :, f * 128:(f + 1) * 128], [128, 32, 128])
            for (c0, n) in T2:
                ob = oc[0] % 4
                oc[0] += 1
                for ki, k in enumerate(list(range(16, 32)) + list(range(16))):
                    rhs = (lambda k=k, c0=c0, n=n: ya_fm[:, k, c0:c0 + n]) if k < 16 else (lambda k=k, c0=c0, n=n: yb_fm[:, k - 16, c0:c0 + n])
                    P.add("pe", lambda e, k=k, ki=ki, wv=wv, rhs=rhs, ob=ob: e.matmul(ps[:, ob, :], lhsT=wv[:, k, :], rhs=rhs(), start=(ki == 0), stop=(ki == 31)),
                          r=[rk, "ya" if k < 16 else ("yb", (k - 16) // 4)], w=[("ps", ob)])
                P.add("dve", lambda e, f=f, c0=c0, n=n, ob=ob: e.tensor_tensor(out=h_fm[:, f, c0:c0 + n], in0=h_fm[:, f, c0:c0 + n], in1=ps[:, ob, :], op=ALU.add),
                      r=[("ps", ob), ("h", f)], w=[("h", f)])
        P.barrier(("pe", "act", "dve", "pool"))
        ring["ns"] = NSLOT

    def ple():
        P.barrier(("pe", "act", "dve", "sp", "pool"))
        rmsnorm(3, T2)
        pst = V(OFF_HID, [128, 8, 256], F32)
        p_fm = V(OFF_HID + 8192, [128, 2, T], BF16)
        P.add("sp", lambda e: e.dma_start(out=pst, in_=p_d.rearrange("(i p) c -> p i c", p=128)), w=["pst"], dma=True, sem="pst")
        for kt in range(2):
            for ih in range(2):
                bk = 4 + (2 * kt + ih) % 2
                for m in range(4):
                    i = 4 * ih + m
                    P.add("pe", lambda e, kt=kt, i=i, m=m, bk=bk: e.transpose(ps[:, bk, m * 128:(m + 1) * 128], pst[:, i, kt * 128:(kt + 1) * 128], idf),
                          r=["pst", "cst"], w=[("ps", bk)])
                P.add("act", lambda e, kt=kt, ih=ih, bk=bk: e.activation(out=p_fm[:, kt, ih * 512:(ih + 1) * 512], in_=ps[:, bk, :], func=AF.Copy), r=[("ps", bk)], w=["p_fm"])
        wp = V(OFF_HID + 12288, [128, 2, D], BF16)
        rp = "wp"
        P.add("pool", lambda e: e.dma_start(out=wp, in_=wpp.rearrange("(kt p) f -> p kt f", p=128)), w=["wp"], dma=True, sem="wp")
        uc = [0]
        for fp in range(8):
            wg, rg = wload(win_cols(wpg, fp * 256), [128, 16, 256])
            for f2 in range(2):
                f = 2 * fp + f2
                for (c0, n) in T2:
                    q = uc[0] % 2
                    uc[0] += 1
                    gb, pb = 2 * q, 2 * q + 1
                    for k in range(16):
                        P.add("pe", lambda e, k=k, wg=wg, f2=f2, c0=c0, n=n, gb=gb: e.matmul(ps[:, gb, 0:n], lhsT=wg[:, k, f2 * 128:(f2 + 1) * 128], rhs=u_fm[:, k, c0:c0 + n],
                                                                                         start=(k == 0), stop=(k == 15)), r=[rg, "u"], w=[("ps", gb)])
                    for k in range(2):
                        P.add("pe", lambda e, k=k, f=f, c0=c0, n=n, pb=pb: e.matmul(ps[:, pb, 0:n], lhsT=wp[:, k, f * 128:(f + 1) * 128], rhs=p_fm[:, k, c0:c0 + n],
                                                                                   start=(k == 0), stop=(k == 1)), r=[rp, "p_fm"], w=[("ps", pb)])
                    P.add("act", lambda e, q=q, n=n, gb=gb: e.activation(out=gtmp[:, q, 0:n], in_=ps[:, gb, 0:n], func=AF.Sigmoid), r=[("ps", gb)], w=[("gtmp", q)])
                    P.add("dve", lambda e, q=q, n=n, pb=pb: e.tensor_tensor(out=gtmp[:, q, 0:n], in0=gtmp[:, q, 0:n], in1=ps[:, pb, 0:n], op=ALU.mult),
                          r=[("gtmp", q), ("ps", pb)], w=[("gtmp", q)])
                    P.add("dve", lambda e, q=q, n=n, f=f, c0=c0: e.tensor_tensor(out=h_fm[:, f, c0:c0 + n], in0=h_fm[:, f, c0:c0 + n], in1=gtmp[:, q, 0:n], op=ALU.add),
                          r=[("gtmp", q), ("h", f)], w=[("h", f)])

    rmsnorm(0, T3)
    ffn(w1i, w1o, T3)

    if stage >= 2 and not skip_mixer:
        mixer()
    if stage >= 3:
        rmsnorm(2, T2)
        ffn(w2i, w2o, T2)
    if stage >= 4:
        ple()

    ot = V(OFF_U, [128, 2, D], F32)
    oc = [0]

    def emit_out_tiles(ti, c0, n):
        for i in range(c0 // 128, (c0 + n) // 128):
            ob = i % 2
            for q in range(4):
                bk = oc[0] % 6
                oc[0] += 1
                for m in range(4):
                    f = 4 * q + m
                    P.add("pe", lambda e, bk=bk, m=m, f=f, i=i: e.transpose(
                        ps[:, bk, m * 128:(m + 1) * 128], h_fm[:, f, i * 128:(i + 1) * 128], idf),
                        r=[("h", f), "cst"], w=[("ps", bk)])
                if oc[0] % 2:
                    P.add("act", lambda e, bk=bk, ob=ob, q=q: e.activation(out=ot[:, ob, q * 512:(q + 1) * 512], in_=ps[:, bk, :], func=AF.Copy),
                          r=[("ps", bk)], w=[("ot", ob, q)])
                else:
                    P.add("dve", lambda e, bk=bk, ob=ob, q=q: e.tensor_copy(ot[:, ob, q * 512:(q + 1) * 512], ps[:, bk, :]),
                          r=[("ps", bk)], w=[("ot", ob, q)])
            P.add("sp", lambda e, i=i, ob=ob: e.dma_start(out=out_d[i * 128:(i + 1) * 128, :], in_=ot[:, ob, :]),
                  r=[("ot", ob, q) for q in range(4)], w=[("outdone", i)], dma=True, sem=("ost", ob))

    if stage >= 5:
        rmsnorm(4, T2, out_fn=lambda f, c0, n: h_fm[:, f, c0:c0 + n], after_tile=emit_out_tiles)
    else:
        P.barrier(("pe", "act", "dve", "sp"))
        for ti, (c0, n) in enumerate(T2):
            emit_out_tiles(ti, c0, n)
    P.add("sp", None, r=[("outdone", i) for i in range(8)])
    P.emit()
    st.close()
    return nc, P.stats


def _host_consts(inputs, core):
    f32 = np.float32
    cst = np.zeros((128, NCST), f32)
    cst[:, C_IDF:C_IDF + 128] = np.eye(128, dtype=f32)
    cst[:, C_ONES:C_ONES + 128] = 1.0
    cst[:, C_TRI:C_TRI + 128] = np.triu(np.ones((128, 128), f32))
    par = np.zeros((128, NPAR), f32)
    norms = [inputs["ffn1_norm"][0], inputs["mix_norm"][0], inputs["ffn2_norm"][0], inputs["ple_norm"][0], inputs["final_norm"]]
    for i, w in enumerate(norms):
        par[:, P_NW + 16 * i:P_NW + 16 * (i + 1)] = np.asarray(w, f32).reshape(16, 128).T
    par[:, P_SCW:P_SCW + 48] = np.asarray(inputs["sc_conv_w"][0], f32).reshape(3, 16, 128).transpose(2, 1, 0).reshape(128, 48)
    par[:, P_SSDW:P_SSDW + 96] = np.asarray(inputs["ssd_conv_w"][0], f32).reshape(4, 24, 128).transpose(2, 1, 0).reshape(128, 96)
    par[:, P_SSDB:P_SSDB + 24] = np.asarray(inputs["ssd_conv_b"][0], f32).reshape(24, 128).T
    par[:, P_DTB:P_DTB + 32] = np.asarray(inputs["ssd_dt_bias"][0], f32)[None, :]
    par[:, P_ALOG:P_ALOG + 32] = np.asarray(inputs["ssd_a_log"][0], f32)[None, :]
    par[:, P_DSK:P_DSK + 32] = np.asarray(inputs["ssd_d"][0], f32)[None, :]
    par[:, P_SSDN:P_SSDN + 16] = np.asarray(inputs["ssd_norm"][0], f32).reshape(16, 128).T
    seq, seg = core // 4, core % 4
    for r in range(8):
        par[:, P_SEG + r] = 1.0 if (r // 4 == seq and r % 4 < seg) else 0.0
    par[:, P_EPS] = EPS
    par[:, P_ONE] = 1.0
    cst[:, C_PAR:] = par
    cstb = np.zeros((128, NCSTB), f32)
    cstb[:, B_ID:B_ID + 128] = np.eye(128, dtype=f32)
    cstb[:, B_MEAN:B_MEAN + 128] = 1.0 / D
    j = np.arange(128)[:, None]
    l = np.arange(128)[None, :]
    nm = np.where(l < j, -1e30, 0.0).astype(f32)
    cstb[:, B_NEG:B_NEG + 512] = np.tile(nm, (1, 4))
    return cst, cstb


_CACHE = {}


def make_in_maps(inputs, ncores=NCORES):
    f32 = np.float32
    x = np.asarray(inputs["x"], f32)
    p = np.asarray(inputs["p"], f32)[0]
    wmi = np.ascontiguousarray(np.asarray(inputs["mix_w_in"], f32)[0])
    wdt = np.ascontiguousarray(wmi[:, 11264:11296].reshape(16, 128, 32).transpose(1, 0, 2).reshape(128, 512))
    shared = {
        "wdt": wdt,
        "ffn1_w_in": np.ascontiguousarray(np.asarray(inputs["ffn1_w_in"], f32)[0]),
        "ffn1_w_out": np.ascontiguousarray(np.asarray(inputs["ffn1_w_out"], f32)[0]),
        "mix_w_in": wmi,
        "mix_w_out": np.ascontiguousarray(np.asarray(inputs["mix_w_out"], f32)[0]),
        "ffn2_w_in": np.ascontiguousarray(np.asarray(inputs["ffn2_w_in"], f32)[0]),
        "ffn2_w_out": np.ascontiguousarray(np.asarray(inputs["ffn2_w_out"], f32)[0]),
        "ple_w_gate": np.ascontiguousarray(np.asarray(inputs["ple_w_gate"], f32)[0]),
        "ple_w_proj": np.ascontiguousarray(np.asarray(inputs["ple_w_proj"], f32)[0]),
    }
    maps = []
    for c in range(ncores):
        seq, seg = c // 4, c % 4
        t0 = seg * T
        xc = np.zeros((TH, D), f32)
        xc[:T] = x[seq, t0:t0 + T]
        if seg > 0:
            xc[T:] = x[seq, t0 - HALO:t0]
        cst, cstb = _host_consts(inputs, c)
        m = {"xc": xc, "pc": np.ascontiguousarray(p[seq, t0:t0 + T]), "cst": cst, "cstb": cstb}
        m.update(shared)
        maps.append(m)
    return maps


def kernel(**inputs):
    if "nc" not in _CACHE:
        _CACHE["nc"] = build_program()[0]
    nc = _CACHE["nc"]
    maps = make_in_maps(inputs)
    res = run_bass_kernel_spmd(nc, maps, core_ids=list(range(NCORES)))
    out = np.zeros((2, 4096, D), np.float32)
    for c in range(NCORES):
        out[c // 4, (c % 4) * T:(c % 4 + 1) * T] = res.results[c]["out"]
    return out
```

```python
import numpy as np
from contextlib import ExitStack
import concourse.bass as bass
import concourse.mybir as mybir
from concourse.bass_utils import run_bass_kernel_spmd

F32 = mybir.dt.float32
BF16 = mybir.dt.bfloat16
U8 = mybir.dt.uint8
AF = mybir.ActivationFunctionType
ALU = mybir.AluOpType

ENGS = ("pe", "act", "dve", "pool", "sp")


class Op:
    __slots__ = ("eng", "fn", "r", "w", "dma", "sem", "deps", "signal", "sigval", "waits", "idx", "bar", "inc")

    def __init__(self, eng, fn, r, w, dma, sem, bar=False, inc=16):
        self.inc = inc
        self.eng = eng
        self.fn = fn
        self.r = tuple(r)
        self.w = tuple(w)
        self.dma = dma
        self.sem = sem
        self.bar = bar
        self.deps = ()
        self.signal = False
        self.sigval = 0
        self.waits = ()


class Prog:
    def __init__(self, nc):
        self.nc = nc
        self.ops = []

    def add(self, eng, fn, r=(), w=(), dma=False, sem=None, inc=16):
        o = Op(eng, fn, r, w, dma, sem, inc=inc)
        self.ops.append(o)
        return o

    def barrier(self, engs=ENGS, dma=True):
        for e in engs:
            o = Op(e, None, (), (), False, None, bar=True)
            o.inc = 1 if dma else 0
            self.ops.append(o)

    def analyze(self):
        last_w = {}
        readers = {}
        last_eng = {}
        last_dma = {}
        for i, o in enumerate(self.ops):
            o.idx = i
            deps = {}
            if o.bar:
                for p in last_eng.values():
                    deps[p.idx] = p
                if o.inc:
                    for p in last_dma.values():
                        deps[p.idx] = p
            for k in o.r:
                p = last_w.get(k)
                if p is not None:
                    deps[p.idx] = p
            for k in o.w:
                p = last_w.get(k)
                if p is not None:
                    deps[p.idx] = p
                rd = readers.get(k)
                if rd:
                    for p in rd.values():
                        deps[p.idx] = p
            for k in o.r:
                rd = readers.setdefault(k, {})
                rd[("d", i) if o.dma else o.eng] = o
            for k in o.w:
                last_w[k] = o
                readers[k] = {}
            deps.pop(i, None)
            dl = []
            for p in deps.values():
                if (not p.dma) and p.eng == "pe" and o.eng == "pe" and not o.dma:
                    continue
                if p.fn is None:
                    continue
                dl.append(p)
                if not p.dma:
                    p.signal = True
            o.deps = dl
            if o.dma:
                last_dma[o.sem] = o
            elif o.fn is not None:
                last_eng[o.eng] = o
        cnt = {e: 0 for e in ENGS}
        dcnt = {}
        for o in self.ops:
            if o.dma:
                dcnt[o.sem] = dcnt.get(o.sem, 0) + o.inc
                o.sigval = dcnt[o.sem]
            elif o.signal:
                cnt[o.eng] += 1
                o.sigval = cnt[o.eng]
        self.dma_keys = list(dcnt.keys())
        waited = {e: {} for e in ENGS}
        nw = 0
        for o in self.ops:
            need = {}
            for p in o.deps:
                key = ("d", p.sem) if p.dma else ("e", p.eng)
                if p.sigval > need.get(key, 0):
                    need[key] = p.sigval
            ws = []
            wd = waited[o.eng]
            for key, v in need.items():
                if v > wd.get(key, 0):
                    wd[key] = v
                    ws.append((key, v))
            o.waits = ws
            nw += len(ws)
        self.stats = dict(n_ops=len(self.ops), n_waits=nw, sig=cnt, ndma_sems=len(dcnt))

    def emit(self):
        nc = self.nc
        self.analyze()
        byeng = {e: [] for e in ENGS}
        for o in self.ops:
            byeng[o.eng].append(o)
        with ExitStack() as st:
            esem = {e: st.enter_context(nc.semaphore("se_" + e)) for e in ENGS}
            dsem = {k: st.enter_context(nc.semaphore("sd_%d" % i)) for i, k in enumerate(self.dma_keys)}
            block = st.enter_context(nc.Block())

            def mk(ename):
                def f(eng):
                    for o in byeng[ename]:
                        for (key, v) in o.waits:
                            s = dsem[key[1]] if key[0] == "d" else esem[key[1]]
                            eng.wait_ge(s, v)
                        if o.fn is None:
                            continue
                        ins = o.fn(eng)
                        if o.dma:
                            ins.then_inc(dsem[o.sem], o.inc)
                        elif o.signal:
                            ins.then_inc(esem[ename], 1)
                return f

            block.tensor(mk("pe"))
            block.scalar(mk("act"))
            block.vector(mk("dve"))
            block.gpsimd(mk("pool"))
            block.sync(mk("sp"))


D = 2048
DT = 16
DFF = 5632
FT = 44
T = 1024
HALO = 8
TH = T + HALO
T3 = [(0, 344), (344, 344), (688, 344)]
T2 = [(0, 512), (512, 512)]
EPS = 1e-6
NCORES = 8

OFF_H = 0
OFF_U = 66048
OFF_HID = 99072
OFF_RING = 132096
NSLOT = 6
SLOT = 8192
OFF_R2 = OFF_RING + 3 * SLOT
OFF_C = OFF_RING + NSLOT * SLOT
OFF_T = OFF_C + 6144
SB_BYTES = 212800

C_IDF = 0
C_ONES = 128
C_TRI = 256
C_PAR = 384
P_NW = 0
P_SCW = 80
P_SSDW = 128
P_SSDB = 224
P_DTB = 248
P_ALOG = 280
P_DSK = 312
P_SSDN = 344
P_SEG = 360
P_EPS = 368
P_ONE = 369
NPAR = 384
NCST = C_PAR + NPAR
B_ID = 0
B_MEAN = 128
B_NEG = 256
NCSTB = 768


def build_program(stage=5, ncores=NCORES, skip_mixer=False):
    nc = bass.Bass("TRN2", target_bir_lowering=False)
    dram = {}

    def din(name, shape):
        dram[name] = nc.dram_tensor(name, shape, F32, kind="ExternalInput").ap()
        return dram[name]

    x_d = din("xc", [TH, D])
    p_d = din("pc", [T, 256])
    cst_d = din("cst", [128, NCST])
    cstb_d = din("cstb", [128, NCSTB])
    wdt_d = din("wdt", [128, 16 * 32])
    w1i = din("ffn1_w_in", [D, 2 * DFF])
    w1o = din("ffn1_w_out", [DFF, D])
    wmi = din("mix_w_in", [D, 11296])
    wmo = din("mix_w_out", [4096, D])
    w2i = din("ffn2_w_in", [D, 2 * DFF])
    w2o = din("ffn2_w_out", [DFF, D])
    wpg = din("ple_w_gate", [D, D])
    wpp = din("ple_w_proj", [256, D])
    out_d = nc.dram_tensor("out", [T, D], F32, kind="ExternalOutput").ap()
    h_sp = nc.dram_tensor("h_spill", [128, DT * T], F32).ap()
    ya_sp = nc.dram_tensor("ya_spill", [128, DT * T], BF16).ap()
    cc_in = nc.dram_tensor("cc_in", [128, 2080], F32)
    cc_out = nc.dram_tensor("cc_out", [ncores * 128, 2080], F32)

    P = Prog(nc)
    st = ExitStack()
    raw = st.enter_context(nc.sbuf_tensor("raw", [128, SB_BYTES], U8))
    ps = st.enter_context(nc.psum_tensor("ps", [128, 8, 512], F32))

    def V(off, shape, dt):
        n = int(np.prod(shape[1:])) * (4 if dt == F32 else 2)
        v = raw[:, off:off + n].bitcast(dt)
        if len(shape) == 3:
            v = v.rearrange("p (a b) -> p a b", a=shape[1])
        elif len(shape) == 4:
            v = v.rearrange("p (a b c) -> p a b c", a=shape[1], b=shape[2])
        return v

    def psb(b):
        return ps[:, b, :].bitcast(BF16)

    h_fm = V(OFF_H, [128, DT, TH], F32)
    u_fm = V(OFF_U, [128, DT, TH], BF16)
    hid = V(OFF_HID, [128, 16, TH], BF16)
    cst = V(OFF_C, [128, NCST], F32)
    cstb = V(OFF_C + 3072, [128, NCSTB], BF16)
    wdt = V(OFF_C + 4608, [128, 16, 32], BF16)
    idf = cst[:, C_IDF:C_IDF + 128]
    ones_f = cst[:, C_ONES:C_ONES + 128]
    tri_f = cst[:, C_TRI:C_TRI + 128]
    par = cst[:, C_PAR:C_PAR + NPAR]
    idb = cstb[:, B_ID:B_ID + 128]
    mean_b = cstb[:, B_MEAN:B_MEAN + 128]
    negm = cstb[:, B_NEG:B_NEG + 512]
    eps_c = par[:, P_EPS:P_EPS + 1]
    one_c = par[:, P_ONE:P_ONE + 1]
    rstd_b = V(OFF_T, [128, TH], F32)
    sqb = V(OFF_T + 4128, [128, 2, TH], BF16)
    stmp = V(OFF_T + 8256, [128, 3, 512], F32)
    gtmp = V(OFF_T + 14400, [128, 2, 512], F32)

    P.add("sp", lambda e: e.dma_start(out=cst, in_=cst_d), w=["cst"], dma=True, sem="cst")
    P.add("pool", lambda e: e.dma_start(out=cstb, in_=cstb_d), w=["cstb"], dma=True, sem="cstb")
    P.add("pool", lambda e: e.dma_start(out=wdt, in_=wdt_d.rearrange("p (a b) -> p a b", a=16)), w=["wdt"], dma=True, sem="wdt")

    ring = {"n": 0, "ns": NSLOT}

    def wload(src_ap, shape):
        s = ring["n"] % ring["ns"]
        ring["n"] += 1
        v = V(OFF_RING + s * SLOT, shape, BF16)
        P.add("pool", lambda e: e.dma_start(out=v, in_=src_ap), w=[("ring", s)], dma=True, sem=("ring", s))
        return v, ("ring", s)

    def win_cols(w, c0, ncol=256):
        return w.rearrange("(kt p) f -> p kt f", p=128)[:, :, c0:c0 + ncol]

    xst = V(OFF_HID, [128, 4, D], F32)
    tcount = [0]
    for i in range(9):
        rows = 128 if i < 8 else HALO
        sb = i % 4
        P.add("sp", lambda e, i=i, rows=rows, sb=sb: e.dma_start(out=xst[0:rows, sb, :], in_=x_d[i * 128:i * 128 + rows, :]),
              w=[("xst", sb)], dma=True, sem=("xst", sb))
        for q in range(4):
            bk = tcount[0] % 6
            tcount[0] += 1
            for m in range(4):
                f = 4 * q + m
                P.add("pe", lambda e, bk=bk, m=m, f=f, rows=rows, sb=sb: e.transpose(
                    ps[:, bk, m * 128:m * 128 + rows], xst[0:rows, sb, f * 128:(f + 1) * 128], idf[0:rows, 0:rows]),
                    r=[("xst", sb), "cst"], w=[("ps", bk)])
            eng = "act" if (tcount[0] % 2) else "dve"
            src = lambda bk=bk, rows=rows: ps[:, bk, :].rearrange("p (m c) -> p m c", m=4)[:, :, 0:rows]
            dst = lambda q=q, i=i, rows=rows: h_fm[:, 4 * q:4 * q + 4, i * 128:i * 128 + rows]
            if eng == "act":
                P.add("act", lambda e, src=src, dst=dst: e.activation(out=dst(), in_=src(), func=AF.Copy),
                      r=[("ps", bk)], w=[("h", 4 * q + m) for m in range(4)])
            else:
                P.add("dve", lambda e, src=src, dst=dst: e.tensor_copy(dst(), src()),
                      r=[("ps", bk)], w=[("h", 4 * q + m) for m in range(4)])

    def rmsnorm(wi, tiles, out_fn=None, w_keys=("u",), after_tile=None):
        for ti, (c0, n) in enumerate(tiles):
            bk = 6 + ti % 2
            for f in range(DT):
                sbuf = f % 2
                P.add("act", lambda e, f=f, c0=c0, n=n, sbuf=sbuf: e.activation(
                    out=sqb[:, sbuf, 0:n], in_=h_fm[:, f, c0:c0 + n], func=AF.Square),
                    r=[("h", f)], w=[("sq", sbuf)])
                P.add("pe", lambda e, f=f, n=n, sbuf=sbuf, bk=bk: e.matmul(
                    ps[:, bk, 0:n], lhsT=mean_b, rhs=sqb[:, sbuf, 0:n], start=(f == 0), stop=(f == DT - 1)),
                    r=[("sq", sbuf), "cstb"], w=[("ps", bk)])
            P.add("act", lambda e, c0=c0, n=n, bk=bk: e.activation(
                out=rstd_b[:, c0:c0 + n], in_=ps[:, bk, 0:n], func=AF.Sqrt, bias=eps_c, scale=1.0),
                r=[("ps", bk), "cst"], w=[("rstd", ti)])
            P.add("dve", lambda e, c0=c0, n=n: e.reciprocal(rstd_b[:, c0:c0 + n], rstd_b[:, c0:c0 + n]),
                  r=[("rstd", ti)], w=[("rstd", ti)])
            for f in range(DT):
                o_ap = (lambda f=f, c0=c0, n=n: u_fm[:, f, c0:c0 + n]) if out_fn is None else (lambda f=f, c0=c0, n=n: out_fn(f, c0, n))
                wk = list(w_keys) if out_fn is None else [("h", f)]
                P.add("dve", lambda e, f=f, c0=c0, n=n, o_ap=o_ap: e.scalar_tensor_tensor(
                    out=o_ap(), in0=h_fm[:, f, c0:c0 + n], scalar=par[:, P_NW + wi * 16 + f:P_NW + wi * 16 + f + 1],
                    in1=rstd_b[:, c0:c0 + n], op0=ALU.mult, op1=ALU.mult),
                    r=[("h", f), ("rstd", ti), "cst"], w=wk)
            if after_tile is not None:
                after_tile(ti, c0, n)

    def ffn(w_in, w_out, tiles, pre=None):
        pre = list(pre or [])

        def next_w(src, shape):
            if pre:
                return pre.pop(0)
            return wload(src, shape)
        blocks = [(b * 8, min(8, FT - b * 8)) for b in range((FT + 7) // 8)]
        unit = [0]
        oacc = [0]

        def emit_in(bi):
            j0, kb = blocks[bi]
            buf = bi % 2
            for pr in range(kb // 2):
                wg, rg = next_w(win_cols(w_in, (j0 + 2 * pr) * 128), [128, 16, 256])
                wu, ru = next_w(win_cols(w_in, DFF + (j0 + 2 * pr) * 128), [128, 16, 256])
                for jj2 in range(2):
                    jj = 2 * pr + jj2
                    for ti, (c0, n) in enumerate(tiles):
                        q = unit[0] % 3
                        unit[0] += 1
                        gb, ub = 2 * q, 2 * q + 1
                        for k in range(16):
                            P.add("pe", lambda e, k=k, wg=wg, jj2=jj2, c0=c0, n=n, gb=gb: e.matmul(
                                ps[:, gb, 0:n], lhsT=wg[:, k, jj2 * 128:(jj2 + 1) * 128], rhs=u_fm[:, k, c0:c0 + n],
                                start=(k == 0), stop=(k == 15)), r=[rg, "u"], w=[("ps", gb)])
                        for k in range(16):
                            P.add("pe", lambda e, k=k, wu=wu, jj2=jj2, c0=c0, n=n, ub=ub: e.matmul(
                                ps[:, ub, 0:n], lhsT=wu[:, k, jj2 * 128:(jj2 + 1) * 128], rhs=u_fm[:, k, c0:c0 + n],
                                start=(k == 0), stop=(k == 15)), r=[ru, "u"], w=[("ps", ub)])
                        P.add("act", lambda e, q=q, n=n, gb=gb: e.activation(out=stmp[:, q, 0:n], in_=ps[:, gb, 0:n], func=AF.Silu),
                              r=[("ps", gb)], w=[("stmp", q)])
                        P.add("dve", lambda e, q=q, n=n, ub=ub, buf=buf, jj=jj, c0=c0: e.tensor_tensor(
                            out=hid[:, buf * 8 + jj, c0:c0 + n], in0=stmp[:, q, 0:n], in1=ps[:, ub, 0:n], op=ALU.mult),
                            r=[("stmp", q), ("ps", ub)], w=[("hid", buf, jj)])

        def emit_out(bi):
            j0, kb = blocks[bi]
            buf = bi % 2
            wos = []
            for pr in range(kb // 2):
                src = w_out.rearrange("(kt p) f -> p kt f", p=128)[:, j0 + 2 * pr:j0 + 2 * pr + 2, :]
                wos.append(wload(src, [128, 2, D]))
            for f in range(DT):
                for ti, (c0, n) in enumerate(tiles):
                    ob = 6 + oacc[0] % 2
                    oacc[0] += 1
                    for kk in range(kb):
                        wv, rk = wos[kk // 2]
                        P.add("pe", lambda e, wv=wv, kk=kk, f=f, c0=c0, n=n, ob=ob, buf=buf, kb=kb: e.matmul(
                            ps[:, ob, 0:n], lhsT=wv[:, kk % 2, f * 128:(f + 1) * 128], rhs=hid[:, buf * 8 + kk, c0:c0 + n],
                            start=(kk == 0), stop=(kk == kb - 1)), r=[rk, ("hid", buf, kk)], w=[("ps", ob)])
                    P.add("dve", lambda e, f=f, c0=c0, n=n, ob=ob: e.scalar_tensor_tensor(
                        out=h_fm[:, f, c0:c0 + n], in0=ps[:, ob, 0:n], scalar=0.5, in1=h_fm[:, f, c0:c0 + n],
                        op0=ALU.mult, op1=ALU.add), r=[("ps", ob), ("h", f)], w=[("h", f)])

        nb = len(blocks)
        emit_in(0)
        for bi in range(nb):
            if bi + 1 < nb:
                emit_in(bi + 1)
            emit_out(bi)


    def bc_mid(ap2, n):
        return ap2.unsqueeze(1).to_broadcast([128, n, ap2.shape[1]])

    def bc_last(ap2, n):
        return ap2.unsqueeze(2).to_broadcast([128, ap2.shape[1], n])

    def mixer():
        X_tm = V(OFF_H, [128, 8, 2048], BF16)
        zs_tm = V(OFF_H + 32768, [128, 8, 2048], BF16)
        yb_fm = V(OFF_U, [128, DT, T], BF16)
        ya_fm = V(OFF_HID, [128, DT, T], BF16)
        B_fm = V(OFF_R2, [128, 4, T], BF16)
        C_fm = V(OFF_R2 + 8192, [128, 4, T], BF16)
        st_bf = V(OFF_R2 + 16384, [128, 2048], BF16)
        dtv = V(OFF_R2 + 20480, [128, 8, 32], F32)
        A_tm = V(OFF_R2 + 21504, [128, 8, 32], F32)
        acs = V(OFF_R2 + 22528, [128, 8, 32], F32)
        tot = V(OFF_R2 + 23552, [128, 8, 32], F32)
        sm = lambda i: V(OFF_T + 1024 * i, [128, 8, 32], F32)
        nacs, eacs, wd, cd, wseg, suf, tmpa, tmpb = [sm(i) for i in range(8)]
        a_b = V(OFF_T + 8192, [128, 32], F32)
        totseg = V(OFF_T + 8320, [128, 32], F32)
        ss4 = V(OFF_T + 8448, [128, 8], F32)
        rs4 = V(OFF_T + 8480, [128, 8], F32)
        cfac = V(OFF_T + 8512, [128, 32], F32)
        A = OFF_HID
        pcb = V(A, [128, TH], F32)
        cvb = V(A + 4128, [128, T], F32)
        xsb = V(A + 8224, [128, 2, T], BF16)
        csb = V(A + 8224, [128, TH], F32)
        yast = V(A + 12352, [128, 2, T], BF16)
        btmA = V(A + 16448, [128, 2, 512], BF16)
        xwA = V(A + 18496, [128, 2, 512], BF16)
        sA = V(A + 20544, [128, 2048], F32)
        dsk = par[:, P_DSK:P_DSK + 32]

        ring["ns"] = 3
        rmsnorm(1, T3)
        P.add("sp", lambda e: e.dma_start(out=h_sp.rearrange("p (f t) -> p f t", f=DT), in_=h_fm[:, :, 0:T]),
              r=[("h", f) for f in range(DT)], w=["hsp"], dma=True, sem="hsp")

        cnt = [0]
        xcnt = [0]

        def conv_tile(src_cols, wbase, ntap, j_w):
            off0 = 8 - (ntap - 1)
            P.add("dve", lambda e: e.tensor_scalar(out=cvb, in0=pcb[:, off0:off0 + T], scalar1=par[:, wbase + j_w * ntap:wbase + j_w * ntap + 1],
                                                   scalar2=None, op0=ALU.mult), r=["pcb", "cst"], w=["cvb"])
            for k in range(1, ntap):
                P.add("dve", lambda e, k=k: e.scalar_tensor_tensor(out=cvb, in0=pcb[:, off0 + k:off0 + k + T],
                                                                   scalar=par[:, wbase + j_w * ntap + k:wbase + j_w * ntap + k + 1], in1=cvb, op0=ALU.mult, op1=ALU.add),
                      r=["pcb", "cvb", "cst"], w=["cvb"])

        def proj3(wv, rk, jj2):
            s3 = cnt[0] % 2
            cnt[0] += 1
            for ti, (c0, n) in enumerate(T3):
                for k in range(16):
                    P.add("pe", lambda e, k=k, ti=ti, c0=c0, n=n, s3=s3: e.matmul(
                        ps[:, 3 * s3 + ti, 0:n], lhsT=wv[:, k, jj2 * 128:(jj2 + 1) * 128], rhs=u_fm[:, k, c0:c0 + n],
                        start=(k == 0), stop=(k == 15)), r=[rk, "u"], w=[("ps", 3 * s3 + ti)])
            return s3

        def evac3(s3, dst, key):
            rr = [("ps", 3 * s3 + t) for t in range(3)]
            P.add("act", lambda e: e.activation(out=dst[:, 8:696].rearrange("p (a b) -> p a b", a=2), in_=ps[:, 3 * s3:3 * s3 + 2, 0:344], func=AF.Copy), r=rr, w=[key])
            P.add("act", lambda e: e.activation(out=dst[:, 696:1032], in_=ps[:, 3 * s3 + 2, 0:336], func=AF.Copy), r=rr, w=[key])
            P.add("act", lambda e: e.activation(out=dst[:, 0:8], in_=ps[:, 3 * s3 + 2, 336:344], func=AF.Copy), r=rr, w=[key])

        jorder = list(range(16, 24)) + list(range(0, 16))
        pend = [None]
        for pi in range(12):
            j0 = jorder[2 * pi]
            if j0 == 0:
                P.barrier(("act", "dve"))
            wv, rk = wload(win_cols(wmi, 8192 + j0 * 128), [128, 16, 256])
            for jj2 in range(2):
                j = j0 + jj2
                s3 = proj3(wv, rk, jj2)
                if pend[0] is not None:
                    pend[0]()
                    pend[0] = None
                evac3(s3, pcb, "pcb")
                conv_tile(None, P_SSDW, 4, j)
                bias = par[:, P_SSDB + j:P_SSDB + j + 1]
                if j >= 20:
                    P.add("act", lambda e, j=j, bias=bias: e.activation(out=C_fm[:, j - 20, :], in_=cvb, func=AF.Silu, bias=bias), r=["cvb", "cst"], w=[("C", j - 20)])
                elif j >= 16:
                    P.add("act", lambda e, j=j, bias=bias: e.activation(out=B_fm[:, j - 16, :], in_=cvb, func=AF.Silu, bias=bias), r=["cvb", "cst"], w=[("B", j - 16)])
                else:
                    xb = xcnt[0] % 2
                    xcnt[0] += 1
                    P.add("act", lambda e, xb=xb, bias=bias: e.activation(out=xsb[:, xb, :], in_=cvb, func=AF.Silu, bias=bias), r=["cvb", "cst"], w=[("xsb", xb)])
                    bk = 6 + xb

                    def emit_tr(j=j, xb=xb, bk=bk):
                        for i in range(8):
                            P.add("pe", lambda e, i=i: e.transpose(psb(bk)[:, i * 128:(i + 1) * 128], xsb[:, xb, i * 128:(i + 1) * 128], idb),
                                  r=[("xsb", xb), "cstb"], w=[("ps", bk)])
                        P.add("dve", lambda e: e.tensor_copy(X_tm[:, :, j * 128:(j + 1) * 128], psb(bk)[:, 0:1024].rearrange("p (i c) -> p i c", i=8)),
                              r=[("ps", bk)], w=[("X", j // 4)])
                    pend[0] = emit_tr
        if pend[0] is not None:
            pend[0]()
            pend[0] = None

        for i in range(8):
            for k in range(16):
                P.add("pe", lambda e, i=i, k=k: e.matmul(ps[:, 7, i * 32:(i + 1) * 32], lhsT=u_fm[:, k, i * 128:(i + 1) * 128], rhs=wdt[:, k, :],
                                                         start=(k == 0), stop=(k == 15)), r=["u", "wdt"], w=[("ps", 7)])
        f256 = lambda v: v.rearrange("p a b -> p (a b)")
        P.add("dve", lambda e: e.tensor_tensor(out=dtv, in0=ps[:, 7, 0:256].rearrange("p (a b) -> p a b", a=8), in1=bc_mid(par[:, P_DTB:P_DTB + 32], 8), op=ALU.add),
              r=[("ps", 7), "cst"], w=["dtv"])
        P.add("act", lambda e: e.activation(out=f256(tmpa), in_=f256(dtv), func=AF.Abs), r=["dtv"], w=["tmpa"])
        P.add("act", lambda e: e.activation(out=f256(tmpa), in_=f256(tmpa), func=AF.Exp, scale=-1.0), r=["tmpa"], w=["tmpa"])
        P.add("act", lambda e: e.activation(out=f256(tmpa), in_=f256(tmpa), func=AF.Ln, bias=one_c, scale=1.0), r=["tmpa", "cst"], w=["tmpa"])
        P.add("dve", lambda e: e.tensor_scalar_max(out=f256(dtv), in0=f256(dtv), scalar1=0.0), r=["dtv"], w=["dtv"])
        P.add("dve", lambda e: e.tensor_tensor(out=f256(dtv), in0=f256(dtv), in1=f256(tmpa), op=ALU.add), r=["dtv", "tmpa"], w=["dtv"])
        P.add("act", lambda e: e.activation(out=a_b, in_=par[:, P_ALOG:P_ALOG + 32], func=AF.Exp), r=["cst"], w=["a_b"])
        P.add("dve", lambda e: e.tensor_scalar(out=a_b, in0=a_b, scalar1=-1.0, scalar2=None, op0=ALU.mult), r=["a_b"], w=["a_b"])
        P.add("dve", lambda e: e.tensor_tensor(out=A_tm, in0=dtv, in1=bc_mid(a_b, 8), op=ALU.mult), r=["dtv", "a_b"], w=["A_tm"])
        for c in range(8):
            P.add("pe", lambda e, c=c: e.matmul(ps[:, 6, c * 32:(c + 1) * 32], lhsT=tri_f, rhs=A_tm[:, c, :], start=True, stop=True), r=["A_tm", "cst"], w=[("ps", 6)])
        for c in range(8):
            P.add("pe", lambda e, c=c: e.matmul(ps[:, 6, 256 + c * 32:256 + (c + 1) * 32], lhsT=ones_f, rhs=A_tm[:, c, :], start=True, stop=True), r=["A_tm", "cst"], w=[("ps", 6)])
        P.add("act", lambda e: e.activation(out=f256(acs), in_=ps[:, 6, 0:256], func=AF.Copy), r=[("ps", 6)], w=["acs"])
        P.add("act", lambda e: e.activation(out=f256(tot), in_=ps[:, 6, 256:512], func=AF.Copy), r=[("ps", 6)], w=["tot"])
        P.add("dve", lambda e: e.tensor_scalar(out=f256(nacs), in0=f256(acs), scalar1=-1.0, scalar2=None, op0=ALU.mult), r=["acs"], w=["nacs"])
        P.add("act", lambda e: e.activation(out=f256(eacs), in_=f256(acs), func=AF.Exp), r=["acs"], w=["eacs"])
        P.add("act", lambda e: e.activation(out=f256(cd), in_=f256(tot), func=AF.Exp), r=["tot"], w=["cd"])
        P.add("dve", lambda e: e.tensor_tensor(out=f256(tmpb), in0=f256(tot), in1=f256(acs), op=ALU.subtract), r=["tot", "acs"], w=["tmpb"])
        P.add("act", lambda e: e.activation(out=f256(wd), in_=f256(tmpb), func=AF.Exp), r=["tmpb"], w=["wd"])
        P.add("dve", lambda e: e.tensor_tensor(out=f256(wd), in0=f256(wd), in1=f256(dtv), op=ALU.mult), r=["wd", "dtv"], w=["wd"])
        P.add("dve", lambda e: e.memset(suf[:, 7, :], 0.0), w=["suf"])
        for c in range(6, -1, -1):
            P.add("dve", lambda e, c=c: e.tensor_tensor(out=suf[:, c, :], in0=suf[:, c + 1, :], in1=tot[:, c + 1, :], op=ALU.add), r=["suf", "tot"], w=["suf"])
        P.add("dve", lambda e: e.tensor_tensor(out=totseg, in0=suf[:, 0, :], in1=tot[:, 0, :], op=ALU.add), r=["suf", "tot"], w=["totseg"])
        P.add("dve", lambda e: e.tensor_tensor(out=f256(tmpb), in0=f256(tmpb), in1=f256(suf), op=ALU.add), r=["tmpb", "suf", "wd"], w=["tmpb"])
        P.add("act", lambda e: e.activation(out=f256(wseg), in_=f256(tmpb), func=AF.Exp), r=["tmpb"], w=["wseg"])
        P.add("dve", lambda e: e.tensor_tensor(out=f256(wseg), in0=f256(wseg), in1=f256(dtv), op=ALU.mult), r=["wseg", "dtv"], w=["wseg"])

        def btm_chunk(c, dstv, key):
            for g in range(4):
                P.add("pe", lambda e, g=g, c=c: e.transpose(psb(7)[:, g * 128:(g + 1) * 128], B_fm[:, g, c * 128:(c + 1) * 128], idb),
                      r=[("B", g), "cstb"], w=[("ps", 7)])
            P.add("act", lambda e: e.activation(out=dstv, in_=psb(7)[:, 0:512], func=AF.Copy), r=[("ps", 7)], w=[key])

        def _chain_phaseA():
            for c in range(8):
                bb = c % 2
                btm_chunk(c, btmA[:, bb, :], ("btmA", bb))
                for g in range(4):
                    xb = (c * 4 + g) % 2
                    P.add("dve", lambda e, c=c, g=g, xb=xb: e.tensor_tensor(
                        out=xwA[:, xb, :].rearrange("p (h d) -> p h d", h=8), in0=X_tm[:, c, g * 512:(g + 1) * 512].rearrange("p (h d) -> p h d", h=8),
                        in1=bc_last(wseg[:, c, 8 * g:8 * g + 8], 64), op=ALU.mult), r=[("X", g), "wseg"], w=[("xwA", xb)])
                    P.add("pe", lambda e, c=c, g=g, xb=xb, bb=bb: e.matmul(ps[:, g, :], lhsT=btmA[:, bb, g * 128:(g + 1) * 128], rhs=xwA[:, xb, :],
                                                                           start=(c == 0), stop=(c == 7)), r=[("btmA", bb), ("xwA", xb)], w=[("ps", g)])
            for g in range(4):
                P.add("act" if g % 2 else "dve",
                      (lambda e, g=g: e.activation(out=sA[:, g * 512:(g + 1) * 512], in_=ps[:, g, :], func=AF.Copy)) if g % 2 else
                      (lambda e, g=g: e.tensor_copy(sA[:, g * 512:(g + 1) * 512], ps[:, g, :])), r=[("ps", g)], w=[("sA", g)])
            cin = cc_in.ap()
            P.add("sp", lambda e: e.dma_start(out=cin[:, 0:2048], in_=sA), r=[("sA", g) for g in range(4)], w=["cin0"], dma=True, sem="cin0")
            P.add("sp", lambda e: e.dma_start(out=cin[:, 2048:2080], in_=totseg), r=["totseg"], w=["cin1"], dma=True, sem="cin1")
            P.add("pool", lambda e: e.collective_compute("AllGather", ALU.bypass, replica_groups=[list(range(ncores))],
                                                         ins=[cc_in.ap().opt()], outs=[cc_out.ap().opt()]),
                  r=["cin0", "cin1"], w=["cout"], dma=True, sem="cc", inc=1)


        def _chain_m1b():
            zc = [0]
            for cb in range(8):
                wv, rk = wload(win_cols(wmi, 6144 + cb * 256), [128, 16, 256])
                for i in range(8):
                    bk = 4 + zc[0] % 3
                    zc[0] += 1
                    for k in range(16):
                        P.add("pe", lambda e, k=k, i=i, bk=bk, wv=wv: e.matmul(ps[:, bk, 0:256], lhsT=u_fm[:, k, i * 128:(i + 1) * 128], rhs=wv[:, k, :],
                                                                             start=(k == 0), stop=(k == 15)), r=[rk, "u"], w=[("ps", bk)])
                    P.add("act", lambda e, i=i, cb=cb, bk=bk: e.activation(out=zs_tm[:, i, cb * 256:(cb + 1) * 256], in_=ps[:, bk, 0:256], func=AF.Silu),
                          r=[("ps", bk)], w=[("zs", cb // 2)])


        main_ops = P.ops
        P.ops = []
        _chain_phaseA()
        chA = P.ops
        P.ops = []
        _chain_m1b()
        chB = P.ops
        P.ops = main_ops
        ia = 0
        for ib, o in enumerate(chB):
            P.ops.append(o)
            if ib % 4 == 3 and ia < len(chA):
                P.ops.append(chA[ia])
                ia += 1
        P.ops.extend(chA[ia:])

        yac = [0]
        for pi in range(8):
            j0 = 2 * pi
            wc, rc = wload(win_cols(wmi, 2048 + j0 * 128), [128, 16, 256])
            wx, rx = wload(win_cols(wmi, 4096 + j0 * 128), [128, 16, 256])
            wb, rb = wload(win_cols(wmi, j0 * 128), [128, 16, 256])
            for jj2 in range(2):
                j = j0 + jj2
                s3 = proj3(wc, rc, jj2)
                evac3(s3, csb, "csb")
                s3 = proj3(wx, rx, jj2)
                rr = [("ps", 3 * s3 + t) for t in range(3)]
                P.add("dve", lambda e, s3=s3: e.tensor_tensor(out=pcb[:, 8:696].rearrange("p (a b) -> p a b", a=2), in0=csb[:, 8:696].rearrange("p (a b) -> p a b", a=2),
                                                              in1=ps[:, 3 * s3:3 * s3 + 2, 0:344], op=ALU.mult), r=rr + ["csb"], w=["pcb"])
                P.add("dve", lambda e, s3=s3: e.tensor_tensor(out=pcb[:, 696:1032], in0=csb[:, 696:1032], in1=ps[:, 3 * s3 + 2, 0:336], op=ALU.mult), r=rr + ["csb"], w=["pcb"])
                P.add("dve", lambda e, s3=s3: e.tensor_tensor(out=pcb[:, 0:8], in0=csb[:, 0:8], in1=ps[:, 3 * s3 + 2, 336:344], op=ALU.mult), r=rr + ["csb"], w=["pcb"])
                conv_tile(None, P_SCW, 3, j)
                s3 = proj3(wb, rb, jj2)
                rr = [("ps", 3 * s3 + t) for t in range(3)]
                yb = yac[0] % 2
                yac[0] += 1
                P.add("dve", lambda e, s3=s3, yb=yb: e.tensor_tensor(out=yast[:, yb, 0:688].rearrange("p (a b) -> p a b", a=2), in0=cvb[:, 0:688].rearrange("p (a b) -> p a b", a=2),
                                                                     in1=ps[:, 3 * s3:3 * s3 + 2, 0:344], op=ALU.mult), r=rr + ["cvb"], w=[("yast", yb)])
                P.add("dve", lambda e, s3=s3, yb=yb: e.tensor_tensor(out=yast[:, yb, 688:1024], in0=cvb[:, 688:1024], in1=ps[:, 3 * s3 + 2, 0:336], op=ALU.mult),
                      r=rr + ["cvb"], w=[("yast", yb)])
                P.add("sp", lambda e, j=j, yb=yb: e.dma_start(out=ya_sp[:, j * T:(j + 1) * T], in_=yast[:, yb, :]), r=[("yast", yb)], w=[("yasp", j)],
                      dma=True, sem=("yast", yb))
        P.barrier(("pe", "act", "dve"))

        state = V(A, [128, 2048], F32)
        xd = V(A + 8192, [128, 2, 512], BF16)
        xdt = V(A + 10240, [128, 2, 512], BF16)
        btmB = V(A + 12288, [128, 2, 512], BF16)
        am = V(A + 14336, [128, 2, 512], F32)
        lt = V(A + 18432, [128, 2, 512], F32)
        mt = V(A + 22528, [128, 2, 512], BF16)
        cbt = V(A + 24576, [128, 512], F32)
        t1 = V(A + 26624, [128, 2, 512], F32)
        ybg = V(A + 30720, [128, 2, 512], BF16)
        lr = V(OFF_U, [128, 3, 2080], F32)
        cout = cc_out.ap()
        P.add("dve", lambda e: e.memset(state, 0.0), w=["state"])
        for n_, r_ in enumerate((0, 1, 2, 4, 5, 6)):
            lb = n_ % 3
            P.add("sp", lambda e, r_=r_, lb=lb: e.dma_start(out=lr[:, lb, :], in_=cout[r_ * 128:(r_ + 1) * 128, :]), r=["cout"], w=[("lr", lb)], dma=True, sem=("lr", lb))
            m_r = par[:, P_SEG + r_:P_SEG + r_ + 1]
            P.add("act", lambda e, lb=lb: e.activation(out=cfac, in_=lr[:, lb, 2048:2080], func=AF.Exp), r=[("lr", lb)], w=["cfac"])
            P.add("dve", lambda e: e.tensor_scalar(out=cfac, in0=cfac, scalar1=-1.0, scalar2=None, op0=ALU.add), r=["cfac"], w=["cfac"])
            P.add("dve", lambda e, m_r=m_r: e.tensor_scalar(out=cfac, in0=cfac, scalar1=m_r, scalar2=None, op0=ALU.mult), r=["cfac", "cst"], w=["cfac"])
            P.add("dve", lambda e: e.tensor_scalar(out=cfac, in0=cfac, scalar1=1.0, scalar2=None, op0=ALU.add), r=["cfac"], w=["cfac"])
            P.add("dve", lambda e: e.tensor_tensor(out=state.rearrange("p (h d) -> p h d", h=32), in0=state.rearrange("p (h d) -> p h d", h=32),
                                                   in1=bc_last(cfac, 64), op=ALU.mult), r=["state", "cfac"], w=["state"])
            P.add("dve", lambda e, lb=lb, m_r=m_r: e.scalar_tensor_tensor(out=state, in0=lr[:, lb, 0:2048], scalar=m_r, in1=state, op0=ALU.mult, op1=ALU.add),
                  r=[("lr", lb), "state", "cst"], w=["state"])
        for g in range(4):
            P.add("act", lambda e, g=g: e.activation(out=st_bf[:, g * 512:(g + 1) * 512], in_=state[:, g * 512:(g + 1) * 512], func=AF.Copy), r=["state"], w=[("sbf", g)])
        P.barrier(("pe", "act", "dve"))

        def emit_group(c, g, ch):
            cs = slice(c * 128, (c + 1) * 128)
            bb = c % 2
            acb, ydb, yob = 1 + ch, 3 + ch, 5 + ch
            Xg = lambda: X_tm[:, c, g * 512:(g + 1) * 512].rearrange("p (h d) -> p h d", h=8)
            P.add("dve", lambda e: e.tensor_tensor(out=xdt[:, ch, :].rearrange("p (h d) -> p h d", h=8), in0=Xg(),
                                                    in1=bc_last(dtv[:, c, 8 * g:8 * g + 8], 64), op=ALU.mult), r=[("X", g), "dtv"], w=[("xdt", ch)])
            P.add("dve", lambda e: e.tensor_tensor(out=xd[:, ch, :].rearrange("p (h d) -> p h d", h=8), in0=Xg(),
                                                    in1=bc_last(wd[:, c, 8 * g:8 * g + 8], 64), op=ALU.mult), r=[("X", g), "wd"], w=[("xd", ch)])
            for q2 in range(2):
                q = 2 * g + q2
                P.add("dve", lambda e, q=q: e.tensor_tensor(out=am[:, ch, :].rearrange("p (h l) -> p h l", h=4), in0=bc_mid(tri_f, 4),
                                                             in1=bc_last(A_tm[:, c, 4 * q:4 * q + 4], 128), op=ALU.mult), r=["A_tm", "cst"], w=[("am", ch)])
                P.add("pe", lambda e: e.matmul(ps[:, acb, :], lhsT=ones_f, rhs=am[:, ch, :], start=True, stop=False), r=[("am", ch), "cst"], w=[("ps", acb)])
                P.add("pe", lambda e: e.matmul(ps[:, acb, :], lhsT=idb, rhs=negm, start=False, stop=True), r=["cstb"], w=[("ps", acb)])
                for hh in range(4):
                    P.add("act", lambda e, q=q, hh=hh: e.activation(out=lt[:, ch, hh * 128:(hh + 1) * 128], in_=ps[:, acb, hh * 128:(hh + 1) * 128],
                                                                    func=AF.Exp, bias=nacs[:, c, 4 * q + hh:4 * q + hh + 1], scale=1.0),
                          r=[("ps", acb), "nacs"], w=[("lt", ch)])
                P.add("dve", lambda e: e.tensor_tensor(out=mt[:, ch, :].rearrange("p (h l) -> p h l", h=4), in0=lt[:, ch, :].rearrange("p (h l) -> p h l", h=4),
                                                       in1=bc_mid(cbt[:, g * 128:(g + 1) * 128], 4), op=ALU.mult), r=[("lt", ch), "cbt"], w=[("mt", ch)])
                for hh in range(4):
                    hl = 4 * q2 + hh
                    P.add("pe", lambda e, hh=hh, hl=hl: e.matmul(ps[:, ydb, hl * 64:(hl + 1) * 64], lhsT=mt[:, ch, hh * 128:(hh + 1) * 128],
                                                                 rhs=xdt[:, ch, hl * 64:(hl + 1) * 64], start=True, stop=True),
                          r=[("mt", ch), ("xdt", ch)], w=[("ps", ydb)])
            P.add("pe", lambda e: e.matmul(ps[:, yob, :], lhsT=C_fm[:, g, cs], rhs=st_bf[:, g * 512:(g + 1) * 512], start=True, stop=True),
                  r=[("C", g), ("sbf", g)], w=[("ps", yob)])
            t3v = lambda: t1[:, ch, :].rearrange("p (h d) -> p h d", h=8)
            P.add("dve", lambda e: e.tensor_tensor(out=t3v(), in0=ps[:, yob, :].rearrange("p (h d) -> p h d", h=8),
                                                   in1=bc_last(eacs[:, c, 8 * g:8 * g + 8], 64), op=ALU.mult), r=[("ps", yob), "eacs"], w=[("t1", ch)])
            P.add("dve", lambda e: e.tensor_tensor(out=t1[:, ch, :], in0=t1[:, ch, :], in1=ps[:, ydb, :], op=ALU.add), r=[("t1", ch), ("ps", ydb)], w=[("t1", ch)])
            P.add("dve", lambda e: e.tensor_tensor(out=lt[:, ch, :].rearrange("p (h d) -> p h d", h=8), in0=Xg(), in1=bc_last(dsk[:, 8 * g:8 * g + 8], 64), op=ALU.mult),
                  r=[("X", g), "cst", ("lt", ch)], w=[("lt", ch)])
            P.add("dve", lambda e: e.tensor_tensor(out=t1[:, ch, :], in0=t1[:, ch, :], in1=lt[:, ch, :], op=ALU.add), r=[("t1", ch), ("lt", ch)], w=[("t1", ch)])
            P.add("dve", lambda e: e.tensor_tensor(out=t1[:, ch, :], in0=t1[:, ch, :], in1=zs_tm[:, c, g * 512:(g + 1) * 512], op=ALU.mult),
                  r=[("t1", ch), ("zs", g)], w=[("t1", ch)])
            P.add("dve", lambda e: e.memset(ss4[:, g:g + 1], 0.0), r=[("ss", g)], w=[("ss", g)])
            P.add("act", lambda e: e.activation(out=lt[:, ch, :], in_=t1[:, ch, :], func=AF.Square, accum_out=ss4[:, g:g + 1]),
                  r=[("t1", ch), ("lt", ch), ("ss", g)], w=[("ss", g), ("lt", ch)])
            P.add("act", lambda e: e.activation(out=rs4[:, g:g + 1], in_=ss4[:, g:g + 1], func=AF.Ln, bias=eps_c, scale=1.0 / 512), r=[("ss", g), "cst"], w=[("rs", g)])
            P.add("act", lambda e: e.activation(out=rs4[:, g:g + 1], in_=rs4[:, g:g + 1], func=AF.Exp, scale=-0.5), r=[("rs", g)], w=[("rs", g)])
            P.add("act", lambda e: e.activation(out=ybg[:, ch, :], in_=t1[:, ch, :], func=AF.Copy, scale=rs4[:, g:g + 1]),
                  r=[("t1", ch), ("rs", g)], w=[("ybg", ch)])
            for m in range(4):
                P.add("pe", lambda e, m=m: e.transpose(psb(acb)[:, m * 128:(m + 1) * 128], ybg[:, ch, m * 128:(m + 1) * 128], idb), r=[("ybg", ch), "cstb"], w=[("ps", acb)])
            for m in range(4):
                P.add("act", lambda e, m=m: e.activation(out=yb_fm[:, 4 * g + m, cs], in_=psb(acb)[:, m * 128:(m + 1) * 128], func=AF.Copy,
                                                         scale=par[:, P_SSDN + 4 * g + m:P_SSDN + 4 * g + m + 1]), r=[("ps", acb), "cst"], w=[("yb", g)])
            if c < 7:
                P.add("pe", lambda e: e.matmul(ps[:, yob, :], lhsT=btmB[:, bb, g * 128:(g + 1) * 128], rhs=xd[:, ch, :], start=True, stop=True),
                      r=[("btmB", bb), ("xd", ch)], w=[("ps", yob)])
                sg = lambda: state[:, g * 512:(g + 1) * 512]
                P.add("dve", lambda e: e.tensor_tensor(out=sg().rearrange("p (h d) -> p h d", h=8), in0=sg().rearrange("p (h d) -> p h d", h=8),
                                                        in1=bc_last(cd[:, c, 8 * g:8 * g + 8], 64), op=ALU.mult), r=["cd", ("st", g)], w=[("st", g)])
                P.add("dve", lambda e: e.tensor_tensor(out=sg(), in0=sg(), in1=ps[:, yob, :], op=ALU.add), r=[("st", g), ("ps", yob)], w=[("st", g)])
                P.add("act", lambda e: e.activation(out=st_bf[:, g * 512:(g + 1) * 512], in_=sg(), func=AF.Copy), r=[("st", g)], w=[("sbf", g)])

        for c in range(8):
            cs = slice(c * 128, (c + 1) * 128)
            bb = c % 2
            btm_chunk(c, btmB[:, bb, :], ("btmB", bb))
            for g in range(4):
                P.add("pe", lambda e, g=g, cs=cs: e.matmul(ps[:, 0, g * 128:(g + 1) * 128], lhsT=B_fm[:, g, cs], rhs=C_fm[:, g, cs], start=True, stop=True),
                      r=[("B", g), ("C", g)], w=[("ps", 0)])
            P.add("act", lambda e: e.activation(out=cbt, in_=ps[:, 0, :], func=AF.Copy), r=[("ps", 0)], w=["cbt"])
            chains = [[], []]
            main_ops = P.ops
            for g in range(4):
                P.ops = []
                emit_group(c, g, g % 2)
                chains[g % 2].extend(P.ops)
            P.ops = main_ops
            n0, n1 = len(chains[0]), len(chains[1])
            SK = 24
            for i in range(max(n0, n1 + SK)):
                if i < n0:
                    P.ops.append(chains[0][i])
                if 0 <= i - SK < n1:
                    P.ops.append(chains[1][i - SK])
        P.barrier(("pe", "act", "dve", "sp"))

        P.add("sp", lambda e: e.dma_start(out=h_fm[:, :, 0:T], in_=h_sp.rearrange("p (f t) -> p f t", f=DT)), r=["hsp"], w=[("h", f) for f in range(DT)], dma=True, sem="hrl")
        P.add("sp", lambda e: e.dma_start(out=ya_fm, in_=ya_sp.rearrange("p (f t) -> p f t", f=DT)), r=[("yasp", j) for j in range(16)], w=["ya"], dma=True, sem="yrl")
        oc = [0]
        for f in range(DT):
            wv, rk = wload(wmo.rearrange("(kt p) f -> p kt f", p=128)[:, :, f * 128:(f + 1) * 128], [128, 32, 128])
            for (c0, n) in T2:
                ob = oc[0] % 4
                oc[0] += 1
                for ki, k in enumerate(list(range(16, 32)) + list(range(16))):
                    rhs = (lambda k=k, c0=c0, n=n: ya_fm[:, k, c0:c0 + n]) if k < 16 else (lambda k=k, c0=c0, n=n: yb_fm[:, k - 16, c0:c0 + n])
                    P.add("pe", lambda e, k=k, ki=ki, wv=wv, rhs=rhs, ob=ob: e.matmul(ps[:, ob, :], lhsT=wv[:, k, :], rhs=rhs(), start=(ki == 0), stop=(ki == 31)),
                          r=[rk, "ya" if k < 16 else ("yb", (k - 16) // 4)], w=[("ps", ob)])
                P.add("dve", lambda e, f=f, c0=c0, n=n, ob=ob: e.tensor_tensor(out=h_fm[:, f, c0:c0 + n], in0=h_fm[:, f, c0:c0 + n], in1=ps[:, ob, :], op=ALU.add),
                      r=[("ps", ob), ("h", f)], w=[("h", f)])
        P.barrier(("pe", "act", "dve"))
        if stage >= 3:
            ffn2_pre.append(wload(win_cols(w2i, 0), [128, 16, 256]))
            ffn2_pre.append(wload(win_cols(w2i, DFF), [128, 16, 256]))
            ffn2_pre.append(wload(win_cols(w2i, 256), [128, 16, 256]))
        P.barrier(("pool",))
        ring["ns"] = NSLOT

    def ple():
        P.barrier(("pe", "act", "dve", "sp", "pool"))
        rmsnorm(3, T2)
        pst = V(OFF_HID, [128, 8, 256], F32)
        p_fm = V(OFF_HID + 8192, [128, 2, T], BF16)
        P.add("sp", lambda e: e.dma_start(out=pst, in_=p_d.rearrange("(i p) c -> p i c", p=128)), w=["pst"], dma=True, sem="pst")
        for kt in range(2):
            for ih in range(2):
                bk = 4 + (2 * kt + ih) % 2
                for m in range(4):
                    i = 4 * ih + m
                    P.add("pe", lambda e, kt=kt, i=i, m=m, bk=bk: e.transpose(ps[:, bk, m * 128:(m + 1) * 128], pst[:, i, kt * 128:(kt + 1) * 128], idf),
                          r=["pst", "cst"], w=[("ps", bk)])
                P.add("act", lambda e, kt=kt, ih=ih, bk=bk: e.activation(out=p_fm[:, kt, ih * 512:(ih + 1) * 512], in_=ps[:, bk, :], func=AF.Copy), r=[("ps", bk)], w=["p_fm"])
        wp = V(OFF_HID + 12288, [128, 2, D], BF16)
        rp = "wp"
        P.add("pool", lambda e: e.dma_start(out=wp, in_=wpp.rearrange("(kt p) f -> p kt f", p=128)), w=["wp"], dma=True, sem="wp")
        uc = [0]
        for fp in range(8):
            wg, rg = wload(win_cols(wpg, fp * 256), [128, 16, 256])
            for f2 in range(2):
                f = 2 * fp + f2
                for (c0, n) in T2:
                    q = uc[0] % 2
                    uc[0] += 1
                    gb, pb = 2 * q, 2 * q + 1
                    for k in range(16):
                        P.add("pe", lambda e, k=k, wg=wg, f2=f2, c0=c0, n=n, gb=gb: e.matmul(ps[:, gb, 0:n], lhsT=wg[:, k, f2 * 128:(f2 + 1) * 128], rhs=u_fm[:, k, c0:c0 + n],
                                                                                         start=(k == 0), stop=(k == 15)), r=[rg, "u"], w=[("ps", gb)])
                    for k in range(2):
                        P.add("pe", lambda e, k=k, f=f, c0=c0, n=n, pb=pb: e.matmul(ps[:, pb, 0:n], lhsT=wp[:, k, f * 128:(f + 1) * 128], rhs=p_fm[:, k, c0:c0 + n],
                                                                                   start=(k == 0), stop=(k == 1)), r=[rp, "p_fm"], w=[("ps", pb)])
                    P.add("act", lambda e, q=q, n=n, gb=gb: e.activation(out=gtmp[:, q, 0:n], in_=ps[:, gb, 0:n], func=AF.Sigmoid), r=[("ps", gb)], w=[("gtmp", q)])
                    P.add("dve", lambda e, q=q, n=n, pb=pb: e.tensor_tensor(out=gtmp[:, q, 0:n], in0=gtmp[:, q, 0:n], in1=ps[:, pb, 0:n], op=ALU.mult),
                          r=[("gtmp", q), ("ps", pb)], w=[("gtmp", q)])
                    P.add("dve", lambda e, q=q, n=n, f=f, c0=c0: e.tensor_tensor(out=h_fm[:, f, c0:c0 + n], in0=h_fm[:, f, c0:c0 + n], in1=gtmp[:, q, 0:n], op=ALU.add),
                          r=[("gtmp", q), ("h", f)], w=[("h", f)])

    ffn2_pre = []
    rmsnorm(0, T3)
    ffn(w1i, w1o, T3)

    if stage >= 2 and not skip_mixer:
        mixer()
    if stage >= 3:
        rmsnorm(2, T2)
        ffn(w2i, w2o, T2, pre=ffn2_pre)
    if stage >= 4:
        ple()

    ot = V(OFF_U, [128, 2, D], F32)
    oc = [0]

    def emit_out_tiles(ti, c0, n):
        for i in range(c0 // 128, (c0 + n) // 128):
            ob = i % 2
            for q in range(4):
                bk = oc[0] % 6
                oc[0] += 1
                for m in range(4):
                    f = 4 * q + m
                    P.add("pe", lambda e, bk=bk, m=m, f=f, i=i: e.transpose(
                        ps[:, bk, m * 128:(m + 1) * 128], h_fm[:, f, i * 128:(i + 1) * 128], idf),
                        r=[("h", f), "cst"], w=[("ps", bk)])
                if oc[0] % 2:
                    P.add("act", lambda e, bk=bk, ob=ob, q=q: e.activation(out=ot[:, ob, q * 512:(q + 1) * 512], in_=ps[:, bk, :], func=AF.Copy),
                          r=[("ps", bk)], w=[("ot", ob, q)])
                else:
                    P.add("dve", lambda e, bk=bk, ob=ob, q=q: e.tensor_copy(ot[:, ob, q * 512:(q + 1) * 512], ps[:, bk, :]),
                          r=[("ps", bk)], w=[("ot", ob, q)])
            P.add("sp", lambda e, i=i, ob=ob: e.dma_start(out=out_d[i * 128:(i + 1) * 128, :], in_=ot[:, ob, :]),
                  r=[("ot", ob, q) for q in range(4)], w=[("outdone", i)], dma=True, sem=("ost", ob))

    if stage >= 5:
        rmsnorm(4, T2, out_fn=lambda f, c0, n: h_fm[:, f, c0:c0 + n], after_tile=emit_out_tiles)
    else:
        P.barrier(("pe", "act", "dve", "sp"))
        for ti, (c0, n) in enumerate(T2):
            emit_out_tiles(ti, c0, n)
    P.add("sp", None, r=[("outdone", i) for i in range(8)])
    P.emit()
    st.close()
    return nc, P.stats


def _host_consts(inputs, core):
    f32 = np.float32
    cst = np.zeros((128, NCST), f32)
    cst[:, C_IDF:C_IDF + 128] = np.eye(128, dtype=f32)
    cst[:, C_ONES:C_ONES + 128] = 1.0
    cst[:, C_TRI:C_TRI + 128] = np.triu(np.ones((128, 128), f32))
    par = np.zeros((128, NPAR), f32)
    norms = [inputs["ffn1_norm"][0], inputs["mix_norm"][0], inputs["ffn2_norm"][0], inputs["ple_norm"][0], inputs["final_norm"]]
    for i, w in enumerate(norms):
        par[:, P_NW + 16 * i:P_NW + 16 * (i + 1)] = np.asarray(w, f32).reshape(16, 128).T
    par[:, P_SCW:P_SCW + 48] = np.asarray(inputs["sc_conv_w"][0], f32).reshape(3, 16, 128).transpose(2, 1, 0).reshape(128, 48)
    par[:, P_SSDW:P_SSDW + 96] = np.asarray(inputs["ssd_conv_w"][0], f32).reshape(4, 24, 128).transpose(2, 1, 0).reshape(128, 96)
    par[:, P_SSDB:P_SSDB + 24] = np.asarray(inputs["ssd_conv_b"][0], f32).reshape(24, 128).T
    par[:, P_DTB:P_DTB + 32] = np.asarray(inputs["ssd_dt_bias"][0], f32)[None, :]
    par[:, P_ALOG:P_ALOG + 32] = np.asarray(inputs["ssd_a_log"][0], f32)[None, :]
    par[:, P_DSK:P_DSK + 32] = np.asarray(inputs["ssd_d"][0], f32)[None, :]
    par[:, P_SSDN:P_SSDN + 16] = np.asarray(inputs["ssd_norm"][0], f32).reshape(16, 128).T
    seq, seg = core // 4, core % 4
    for r in range(8):
        par[:, P_SEG + r] = 1.0 if (r // 4 == seq and r % 4 < seg) else 0.0
    par[:, P_EPS] = EPS
    par[:, P_ONE] = 1.0
    cst[:, C_PAR:] = par
    cstb = np.zeros((128, NCSTB), f32)
    cstb[:, B_ID:B_ID + 128] = np.eye(128, dtype=f32)
    cstb[:, B_MEAN:B_MEAN + 128] = 1.0 / D
    j = np.arange(128)[:, None]
    l = np.arange(128)[None, :]
    nm = np.where(l < j, -1e30, 0.0).astype(f32)
    cstb[:, B_NEG:B_NEG + 512] = np.tile(nm, (1, 4))
    return cst, cstb


_CACHE = {}


def make_in_maps(inputs, ncores=NCORES):
    f32 = np.float32
    x = np.asarray(inputs["x"], f32)
    p = np.asarray(inputs["p"], f32)[0]
    wmi = np.ascontiguousarray(np.asarray(inputs["mix_w_in"], f32)[0])
    wdt = np.ascontiguousarray(wmi[:, 11264:11296].reshape(16, 128, 32).transpose(1, 0, 2).reshape(128, 512))
    shared = {
        "wdt": wdt,
        "ffn1_w_in": np.ascontiguousarray(np.asarray(inputs["ffn1_w_in"], f32)[0]),
        "ffn1_w_out": np.ascontiguousarray(np.asarray(inputs["ffn1_w_out"], f32)[0]),
        "mix_w_in": wmi,
        "mix_w_out": np.ascontiguousarray(np.asarray(inputs["mix_w_out"], f32)[0]),
        "ffn2_w_in": np.ascontiguousarray(np.asarray(inputs["ffn2_w_in"], f32)[0]),
        "ffn2_w_out": np.ascontiguousarray(np.asarray(inputs["ffn2_w_out"], f32)[0]),
        "ple_w_gate": np.ascontiguousarray(np.asarray(inputs["ple_w_gate"], f32)[0]),
        "ple_w_proj": np.ascontiguousarray(np.asarray(inputs["ple_w_proj"], f32)[0]),
    }
    maps = []
    for c in range(ncores):
        seq, seg = c // 4, c % 4
        t0 = seg * T
        xc = np.zeros((TH, D), f32)
        xc[:T] = x[seq, t0:t0 + T]
        if seg > 0:
            xc[T:] = x[seq, t0 - HALO:t0]
        cst, cstb = _host_consts(inputs, c)
        m = {"xc": xc, "pc": np.ascontiguousarray(p[seq, t0:t0 + T]), "cst": cst, "cstb": cstb}
        m.update(shared)
        maps.append(m)
    return maps


def kernel(**inputs):
    if "nc" not in _CACHE:
        _CACHE["nc"] = build_program()[0]
    nc = _CACHE["nc"]
    maps = make_in_maps(inputs)
    res = run_bass_kernel_spmd(nc, maps, core_ids=list(range(NCORES)))
    out = np.zeros((2, 4096, D), np.float32)
    for c in range(NCORES):
        out[c // 4, (c % 4) * T:(c % 4 + 1) * T] = res.results[c]["out"]
    return out
```
